# Optimizing a Trainium2 kernel written in Bass

```python
import math
import jax
import jax.numpy as jnp
from jax import lax
import numpy as np

D_MODEL = 1024
BATCH = 1
SEQ = 16384
DEPTH = 4

GRID_W = 64
CTX_LEN = 256
N_MIXERS = 4
HEAD_DIM = 64
ROPE_BASE = 10000.0
RMS_EPS = 1e-6
D_FF = ((8 * D_MODEL // 3 + 255) // 256) * 256
N_MOD = 6

DA_HEADS = D_MODEL // (2 * HEAD_DIM)
DA_VDIM = 2 * HEAD_DIM
Q_BLOCK = 128

SC_WIDTH = 3

WA_HEADS = D_MODEL // HEAD_DIM
WA_KV_HEADS = 4
WINDOW = 128
W_BLOCK = WINDOW

HG_EXPAND = 128
HG_HEADS = D_MODEL // HG_EXPAND
HG_DK = HG_EXPAND
HG_DV = D_MODEL // HG_HEADS
HG_FDIM = HG_HEADS * HG_DK
CHUNK = 64

kernel_name = 'hybrid_interleaved_diffusion_block'


def n_layers_of(mixer):
    return len(range(mixer, DEPTH, N_MIXERS))


def rmsnorm(x, g):
    x32 = x.astype(jnp.float32)
    y = x32 * lax.rsqrt(jnp.mean(x32 * x32, axis=-1, keepdims=True) + RMS_EPS)
    return y.astype(x.dtype) * g


def modulate(x, shift, scale):
    return x * (1.0 + scale) + shift


def swiglu(t, w_in, w_out):
    a, b = jnp.split(t @ w_in, 2, axis=-1)
    return (jax.nn.silu(a) * b) @ w_out


def rope_tables(n_tokens):
    rows = n_tokens // GRID_W
    row = jnp.repeat(jnp.arange(rows), GRID_W).astype(jnp.float32)
    col = jnp.tile(jnp.arange(GRID_W), rows).astype(jnp.float32)
    half = HEAD_DIM // 2
    inv_freq = 1.0 / (ROPE_BASE ** (jnp.arange(0, half, 2, dtype=jnp.float32) / half))
    ang = jnp.concatenate([row[:, None] * inv_freq, col[:, None] * inv_freq], axis=-1)
    return jnp.cos(ang), jnp.sin(ang)


def apply_rope_2d(x, cos, sin):
    n_mid = x.ndim - 3
    qd = HEAD_DIM // 4
    bshape = (cos.shape[0],) + (1,) * n_mid + (2, qd)
    cs, sn = cos.reshape(bshape), sin.reshape(bshape)
    xs = x.reshape(x.shape[:-1] + (2, 2, qd))
    x1, x2 = xs[..., 0, :], xs[..., 1, :]
    out = jnp.stack([x1 * cs - x2 * sn, x2 * cs + x1 * sn], axis=-2)
    return out.reshape(x.shape)


def sink_softmax(scores, sink):
    m = sink
    for s in scores:
        m = jnp.maximum(m, jnp.max(s, axis=-1, keepdims=True))
    es = [jnp.exp(s - m) for s in scores]
    den = jnp.exp(sink - m)
    for e in es:
        den = den + jnp.sum(e, axis=-1, keepdims=True)
    return [e / den for e in es]


def diff_attention(h, hc, wqkv, lam, subln_g, wo, cos, sin, layer_idx, need_ctx):
    B, L, _ = h.shape
    lam_init = 0.8 - 0.6 * math.exp(-0.3 * layer_idx)
    lam32 = lam.astype(jnp.float32)
    lam_full = (jnp.exp(jnp.sum(lam32[0] * lam32[1]))
                - jnp.exp(jnp.sum(lam32[2] * lam32[3])) + lam_init)
    scale = HEAD_DIM ** -0.5

    def project(t):
        n = t.shape[1]
        q, k, v = jnp.split(t @ wqkv, 3, axis=-1)
        return (q.reshape(B, n, DA_HEADS, 2, HEAD_DIM),
                k.reshape(B, n, DA_HEADS, 2, HEAD_DIM),
                v.reshape(B, n, DA_HEADS, DA_VDIM))

    def attend(qb, kk, vv):
        s = jnp.einsum('bqhmd,bkhmd->bhmqk', qb, kk).astype(jnp.float32) * scale
        p = jax.nn.softmax(s, axis=-1)
        p = p[:, :, 0] - lam_full * p[:, :, 1]
        return jnp.einsum('bhqk,bkhe->bqhe', p.astype(vv.dtype), vv)

    def readout(o):
        n = o.shape[1]
        o = rmsnorm(o, subln_g) * (1.0 - lam_init)
        return o.reshape(B, n, D_MODEL) @ wo

    q, k, v = project(h)
    q, k = apply_rope_2d(q, cos, sin), apply_rope_2d(k, cos, sin)
    qc, kc, vc = project(hc)
    k_all = jnp.concatenate([kc, k], axis=1)
    v_all = jnp.concatenate([vc, v], axis=1)
    nb = L // Q_BLOCK
    q_blocks = jnp.moveaxis(q.reshape(B, nb, Q_BLOCK, DA_HEADS, 2, HEAD_DIM), 1, 0)
    o = lax.map(lambda qb: attend(qb, k_all, v_all), q_blocks)
    o = jnp.moveaxis(o, 0, 1).reshape(B, L, DA_HEADS, DA_VDIM)
    y = readout(o)
    yc = readout(attend(qc, kc, vc)) if need_ctx else None
    return yc, y


def short_conv(h, hc, w_in, conv_w, w_out, need_ctx):
    pad = SC_WIDTH // 2

    def run(t):
        n = t.shape[1]
        b_gate, c_gate, u = jnp.split(t @ w_in, 3, axis=-1)
        z = c_gate * u
        zp = jnp.pad(z, ((0, 0), (pad, pad), (0, 0)))
        conv = conv_w[0] * zp[:, 0:n]
        for j in range(1, SC_WIDTH):
            conv = conv + conv_w[j] * zp[:, j:j + n]
        return (b_gate * conv) @ w_out

    y = run(h)
    yc = run(hc) if need_ctx else None
    return yc, y


def window_attention(h, hc, wqkv, sinks, wo, cos, sin, need_ctx):
    B, L, _ = h.shape
    G, R = WA_KV_HEADS, WA_HEADS // WA_KV_HEADS
    scale = HEAD_DIM ** -0.5
    qw, kw = WA_HEADS * HEAD_DIM, WA_KV_HEADS * HEAD_DIM

    def project(t):
        n = t.shape[1]
        qkv = t @ wqkv
        q = qkv[..., :qw].reshape(B, n, G, R, HEAD_DIM)
        k = qkv[..., qw:qw + kw].reshape(B, n, G, HEAD_DIM)
        v = qkv[..., qw + kw:].reshape(B, n, G, HEAD_DIM)
        return q, k, v

    sink = sinks.reshape(G, R).astype(jnp.float32)
    q, k, v = project(h)
    q, k = apply_rope_2d(q, cos, sin), apply_rope_2d(k, cos, sin)
    qc, kc, vc = project(hc)

    nb = L // W_BLOCK
    qb = q.reshape(B, nb, W_BLOCK, G, R, HEAD_DIM)

    def band(a):
        ab = jnp.pad(a.reshape(B, nb, W_BLOCK, G, HEAD_DIM), ((0, 0), (1, 1), (0, 0), (0, 0), (0, 0)))
        return jnp.concatenate([ab[:, :-2], ab[:, 1:-1], ab[:, 2:]], axis=2)

    kb, vb = band(k), band(v)
    q_pos = jnp.arange(nb)[:, None] * W_BLOCK + jnp.arange(W_BLOCK)[None, :]
    k_pos = (jnp.arange(nb)[:, None] - 1) * W_BLOCK + jnp.arange(3 * W_BLOCK)[None, :]
    rel = k_pos[:, None, :] - q_pos[:, :, None]
    valid = (jnp.abs(rel) <= WINDOW) & (k_pos[:, None, :] >= 0) & (k_pos[:, None, :] < L)

    s_win = jnp.einsum('bnqgrd,bnkgd->bngrqk', qb, kb).astype(jnp.float32) * scale
    s_win = jnp.where(valid[None, :, None, None], s_win, -jnp.inf)
    s_ctx = jnp.einsum('bnqgrd,bcgd->bngrqc', qb, kc).astype(jnp.float32) * scale
    p_ctx, p_win = sink_softmax([s_ctx, s_win], sink[None, None, :, :, None, None])
    o = (jnp.einsum('bngrqc,bcgd->bnqgrd', p_ctx.astype(vc.dtype), vc)
         + jnp.einsum('bngrqk,bnkgd->bnqgrd', p_win.astype(vb.dtype), vb))
    y = o.reshape(B, L, D_MODEL) @ wo

    yc = None
    if need_ctx:
        n = hc.shape[1]
        s_c = jnp.einsum('bqgrd,bcgd->bgrqc', qc, kc).astype(jnp.float32) * scale
        (p_c,) = sink_softmax([s_c], sink[None, :, :, None, None])
        oc = jnp.einsum('bgrqc,bcgd->bqgrd', p_c.astype(vc.dtype), vc)
        yc = oc.reshape(B, n, D_MODEL) @ wo
    return yc, y


def gla_scan(q, k, v, logf, s0, with_output):
    B, NH, T, _ = q.shape
    nc = T // CHUNK

    def chunks(a):
        return jnp.moveaxis(a.astype(jnp.float32).reshape(B, NH, nc, CHUNK, a.shape[-1]), 2, 0)

    tri = jnp.tril(jnp.ones((CHUNK, CHUNK), dtype=bool))[:, :, None]

    def step(S, xs):
        qc, kc, vc, gc = xs
        b = jnp.cumsum(gc, axis=2)
        b_end = b[:, :, -1:, :]
        S_next = (jnp.exp(b_end)[:, :, 0, :, None] * S
                  + jnp.einsum('bhsk,bhsv->bhkv', kc * jnp.exp(b_end - b), vc))
        if not with_output:
            return S_next, None
        rel = jnp.where(tri, b[:, :, :, None, :] - b[:, :, None, :, :], -jnp.inf)
        att = jnp.sum(qc[:, :, :, None, :] * kc[:, :, None, :, :] * jnp.exp(rel), axis=-1)
        o = att @ vc + jnp.einsum('bhtk,bhkv->bhtv', qc * jnp.exp(b), S)
        return S_next, o

    S_fin, o = lax.scan(step, s0, (chunks(q), chunks(k), chunks(v), chunks(logf)))
    if not with_output:
        return S_fin, None
    return S_fin, jnp.moveaxis(o, 0, 2).reshape(B, NH, T, v.shape[-1])


def hgrn2(h, hc, w_in, lb_param, gnorm_g, wo, layer_idx, need_ctx):
    B = h.shape[0]
    H = HG_HEADS
    p = jax.nn.softmax(lb_param.astype(jnp.float32), axis=1)
    lb = (jnp.cumsum(p, axis=1) - p[:, :1])[:, layer_idx][:, None, None, :]

    def gates(t):
        q, f_fw, f_bw, i_in, og = jnp.split(t @ w_in, 5, axis=-1)
        q = jax.nn.silu(q)
        f_pre = jnp.stack([f_fw, f_bw], axis=0).astype(jnp.float32)
        logf = jnp.logaddexp(jnp.log(lb), jnp.log1p(-lb) + jax.nn.log_sigmoid(f_pre))
        k = -jnp.expm1(logf)
        return q, k, i_in, logf, og

    def to_heads(a):
        return a.reshape(a.shape[0], a.shape[1], H, -1).transpose(0, 2, 1, 3)

    def both(a_fw, a_bw):
        return jnp.concatenate([to_heads(a_fw), jnp.flip(to_heads(a_bw), axis=2)], axis=1)

    def merge(o):
        return (o[:, :H] + jnp.flip(o[:, H:], axis=2)).transpose(0, 2, 1, 3)

    def readout(o, og):
        n = o.shape[1]
        o = rmsnorm(o.astype(h.dtype), gnorm_g) * jax.nn.silu(og).reshape(B, n, H, HG_DV)
        return o.reshape(B, n, D_MODEL) @ wo

    qc, kc, vc, gc, ogc = gates(hc)
    s0 = jnp.zeros((B, 2 * H, HG_DK, HG_DV), jnp.float32)
    S_ctx, o_ctx = gla_scan(both(qc, qc), both(kc[0], kc[1]), both(vc, vc), both(gc[0], gc[1]),
                            s0, need_ctx)
    q, k, v, g, og = gates(h)
    _, o = gla_scan(both(q, q), both(k[0], k[1]), both(v, v), both(g[0], g[1]), S_ctx, True)
    y = readout(merge(o), og)
    yc = readout(merge(o_ctx), ogc) if need_ctx else None
    return yc, y


def setup_inputs(seed: int = 0) -> dict:
    key = jax.random.key(seed)
    keys = jax.random.split(key, 32)
    counter = [0]

    def nrm(shape, scale):
        k = keys[counter[0]]
        counter[0] += 1
        return jax.random.normal(k, shape, jnp.float32) * scale

    D, F = D_MODEL, D_FF
    nA, nB, nC, nD = n_layers_of(0), n_layers_of(1), n_layers_of(2), n_layers_of(3)
    wa_cols = (WA_HEADS + 2 * WA_KV_HEADS) * HEAD_DIM
    return {
        'x': nrm((BATCH, SEQ, D), 1.0),
        'c': nrm((BATCH, D), 1.0),
        'ctx': nrm((BATCH, CTX_LEN, D), 1.0),
        'c_ctx': nrm((D,), 1.0),
        'ada_w': nrm((DEPTH, D, N_MOD * D), 0.5 * D ** -0.5),
        'ada_b': nrm((DEPTH, N_MOD * D), 0.02),
        'norm1_g': 1.0 + nrm((DEPTH, D), 0.02),
        'norm2_g': 1.0 + nrm((DEPTH, D), 0.02),
        'ffn_w_in': nrm((DEPTH, D, 2 * F), D ** -0.5),
        'ffn_w_out': nrm((DEPTH, F, D), F ** -0.5),
        'final_g': 1.0 + nrm((D,), 0.02),
        'da_wqkv': nrm((nA, D, 3 * D), D ** -0.5),
        'da_lambda': nrm((nA, 4, HEAD_DIM), 0.1),
        'da_subln_g': 1.0 + nrm((nA, DA_VDIM), 0.02),
        'da_wo': nrm((nA, D, D), D ** -0.5),
        'sc_w_in': nrm((nB, D, 3 * D), D ** -0.5),
        'sc_conv_w': nrm((nB, SC_WIDTH, D), SC_WIDTH ** -0.5),
        'sc_w_out': nrm((nB, D, D), D ** -0.5),
        'wa_wqkv': nrm((nC, D, wa_cols), D ** -0.5),
        'wa_sinks': nrm((nC, WA_HEADS), 0.5),
        'wa_wo': nrm((nC, WA_HEADS * HEAD_DIM, D), D ** -0.5),
        'hg_w_in': nrm((nD, D, 5 * D), D ** -0.5),
        'hg_lb': nrm((2, DEPTH, HG_FDIM), 0.1),
        'hg_gnorm_g': 1.0 + nrm((nD, HG_DV), 0.02),
        'hg_wo': nrm((nD, D, D), D ** -0.5),
    }


def reference(x, c, ctx, c_ctx, ada_w, ada_b, norm1_g, norm2_g, ffn_w_in, ffn_w_out, final_g,
              da_wqkv, da_lambda, da_subln_g, da_wo, sc_w_in, sc_conv_w, sc_w_out,
              wa_wqkv, wa_sinks, wa_wo, hg_w_in, hg_lb, hg_gnorm_g, hg_wo):
    B, L, _ = x.shape
    cos, sin = rope_tables(L)
    cos, sin = cos.astype(x.dtype), sin.astype(x.dtype)
    silu_c = jax.nn.silu(c)
    silu_cc = jax.nn.silu(c_ctx)[None, :]
    for i in range(DEPTH):
        mixer, slot = i % N_MIXERS, i // N_MIXERS
        need_ctx = i < DEPTH - 1
        mod = (silu_c @ ada_w[i] + ada_b[i])[:, None, :]
        mod_c = (silu_cc @ ada_w[i] + ada_b[i])[:, None, :]
        sh1, sc1, g1, sh2, sc2, g2 = jnp.split(mod, N_MOD, axis=-1)
        csh1, csc1, cg1, csh2, csc2, cg2 = jnp.split(mod_c, N_MOD, axis=-1)
        h = modulate(rmsnorm(x, norm1_g[i]), sh1, sc1)
        hc = modulate(rmsnorm(ctx, norm1_g[i]), csh1, csc1)
        if mixer == 0:
            yc, y = diff_attention(h, hc, da_wqkv[slot], da_lambda[slot], da_subln_g[slot],
                                   da_wo[slot], cos, sin, i, need_ctx)
        elif mixer == 1:
            yc, y = short_conv(h, hc, sc_w_in[slot], sc_conv_w[slot], sc_w_out[slot], need_ctx)
        elif mixer == 2:
            yc, y = window_attention(h, hc, wa_wqkv[slot], wa_sinks[slot], wa_wo[slot],
                                     cos, sin, need_ctx)
        else:
            yc, y = hgrn2(h, hc, hg_w_in[slot], hg_lb, hg_gnorm_g[slot], hg_wo[slot], i, need_ctx)
        x = x + g1 * y
        x = x + g2 * swiglu(modulate(rmsnorm(x, norm2_g[i]), sh2, sc2), ffn_w_in[i], ffn_w_out[i])
        if need_ctx:
            ctx = ctx + cg1 * yc
            ctx = ctx + cg2 * swiglu(modulate(rmsnorm(ctx, norm2_g[i]), csh2, csc2),
                                     ffn_w_in[i], ffn_w_out[i])
    return rmsnorm(x, final_g)
```

```python
import math
from contextlib import ExitStack
import numpy as np
import ml_dtypes
import concourse.bass as bass
import concourse.mybir as mybir
from concourse.bass_utils import run_bass_kernel_spmd

F32 = mybir.dt.float32
BF16 = mybir.dt.bfloat16
AF = mybir.ActivationFunctionType
ALU = mybir.AluOpType
NPBF = ml_dtypes.bfloat16

NCORES = 8
D = 1024
SEQ = 16384
NT = SEQ // NCORES
NCX = 256
DFF = 2816
EPS = 1e-6
PE, ACT, DVE, POOL, SP = "pe", "act", "dve", "pool", "sp"
ENGS = (PE, ACT, DVE, POOL, SP)


class Sem:
    def __init__(self, h):
        self.h = h
        self.n = 0


class Trk:
    __slots__ = ("w", "r", "dsem")

    def __init__(self):
        self.w = {}
        self.r = {}
        self.dsem = None


class Builder:
    def __init__(self):
        self.nc = bass.Bass("TRN2", target_bir_lowering=False)
        self.es = ExitStack()
        self.prog = {e: [] for e in ENGS}
        self.seen = {e: {} for e in ENGS}
        self.nsem = 0
        self.esem = {e: self.newsem(e) for e in (PE, ACT, DVE, POOL)}
        self.dsems = []
        self.same_sync = True
        self.uid = 0
        self.outs = []
        self.pst = [self.es.enter_context(self.nc.psum_tensor(f"ps{i}", [128, 1024], F32)) for i in range(4)]
        self.bank_trk = [Trk() for _ in range(8)]
        self.bank_i = 0

    def newsem(self, name):
        self.nsem += 1
        return Sem(self.es.enter_context(self.nc.semaphore(f"s{self.nsem}_{name}")))

    def sb(self, shape, dt, name=None, es=None):
        self.uid += 1
        if es is not None and hasattr(es, "get"):
            return es.get(shape, dt)
        return (es or self.es).enter_context(self.nc.sbuf_tensor(f"{name or 't'}{self.uid}", list(shape), dt))

    def din(self, name, shape, dt=F32):
        return self.nc.dram_tensor(name, list(shape), dt, kind="ExternalInput").ap()

    def dout(self, name, shape, dt=F32):
        ap = self.nc.dram_tensor(name, list(shape), dt, kind="ExternalOutput").ap()
        t = Trk()
        self.outs.append(t)
        return ap, t

    def dscr(self, name, shape, dt=BF16):
        return self.nc.dram_tensor(name, list(shape), dt, kind="Internal").ap()

    def bank(self, i):
        return self.pst[i // 2][:, (i % 2) * 512:(i % 2) * 512 + 512], self.bank_trk[i]

    def nextbank(self, lo=0, hi=8):
        i = lo + (self.bank_i % (hi - lo))
        self.bank_i += 1
        return self.bank(i)

    def _wait(self, eng, sp):
        sem, val = sp
        if sem is self.esem.get(eng) and (eng == PE or not self.same_sync):
            return
        if self.seen[eng].get(sem, 0) >= val:
            return
        self.seen[eng][sem] = val
        self.prog[eng].append(lambda e, s=sem.h, v=val: e.wait_ge(s, v))

    def _deps(self, eng, reads, writes):
        for t in reads:
            for sp in t.w.values():
                self._wait(eng, sp)
        for t in writes:
            for sp in t.w.values():
                self._wait(eng, sp)
            for sp in t.r.values():
                self._wait(eng, sp)

    def op(self, eng, fn, reads=(), writes=(), inc=True):
        self._deps(eng, reads, writes)
        sem = self.esem[eng]
        if inc:
            sem.n += 1
            v = sem.n
            self.prog[eng].append(lambda e, f=fn, s=sem.h: f(e).then_inc(s, 1))
        else:
            v = sem.n + 1
            self.prog[eng].append(lambda e, f=fn: f(e))
        sp = (sem, v)
        for t in reads:
            t.r[eng] = sp
        for t in writes:
            t.w[sem] = sp
            t.r = {}

    def dma(self, q, out, in_, reads=(), writes=(), st=None):
        self._deps(q, reads, writes)
        st = st or (writes[0] if writes else reads[0])
        if st.dsem is None:
            st.dsem = self.newsem("d")
            self.dsems.append(st.dsem)
        ds = st.dsem
        ds.n += 16
        sp = (ds, ds.n)
        self.prog[q].append(lambda e, o=out, i=in_, s=ds.h: e.dma_start(out=o, in_=i).then_inc(s, 16))
        for t in reads:
            t.r[ds] = sp
        for t in writes:
            t.w[ds] = sp
            t.r = {}

    def barrier(self):
        for e in ENGS:
            for e2 in (PE, ACT, DVE, POOL):
                if e2 != e and self.esem[e2].n > 0:
                    self._wait(e, (self.esem[e2], self.esem[e2].n))
            for ds in self.dsems:
                if ds.n > 0:
                    self._wait(e, (ds, ds.n))

    def mm(self, out, lhsT, rhs, start, stop, reads, writes, inc):
        self.op(PE, lambda e, o=out, l=lhsT, r=rhs, a=start, b=stop: e.matmul(o, l, r, start=a, stop=b),
                reads=reads, writes=writes, inc=inc)

    def act(self, out, in_, func, reads, writes, bias=0.0, scale=1.0, accum_out=None):
        def f(e, o=out, i=in_, fu=func, b=bias, s=scale, a=accum_out):
            if a is None:
                return e.activation(out=o, in_=i, func=fu, bias=b, scale=s)
            return e.activation(out=o, in_=i, func=fu, bias=b, scale=s, accum_out=a)
        self.op(ACT, f, reads=reads, writes=writes)

    def tt(self, eng, out, in0, in1, op, reads, writes):
        self.op(eng, lambda e, o=out, a=in0, b=in1, p=op: e.tensor_tensor(o, a, b, p), reads=reads, writes=writes)

    def ts(self, eng, out, in0, s1, s2, op0, op1, reads, writes):
        if s2 is None:
            self.op(eng, lambda e, o=out, a=in0, x=s1, p=op0: e.tensor_scalar(o, a, x, None, p),
                    reads=reads, writes=writes)
        else:
            self.op(eng, lambda e, o=out, a=in0, x=s1, y=s2, p=op0, q=op1: e.tensor_scalar(o, a, x, y, p, q),
                    reads=reads, writes=writes)

    def stt(self, eng, out, in0, scalar, in1, op0, op1, reads, writes):
        self.op(eng, lambda e, o=out, a=in0, s=scalar, b=in1, p=op0, q=op1: e.scalar_tensor_tensor(o, a, s, b, p, q),
                reads=reads, writes=writes)

    def copy(self, eng, out, in_, reads, writes):
        self.op(eng, lambda e, o=out, i=in_: e.tensor_copy(o, i), reads=reads, writes=writes)

    def recip(self, out, in_, reads, writes):
        self.op(DVE, lambda e, o=out, i=in_: e.reciprocal(o, i), reads=reads, writes=writes)

    def memset(self, eng, ap, val, writes):
        self.op(eng, lambda e, a=ap, v=val: e.memset(a, v), writes=writes)

    def finish(self):
        for t in self.outs:
            for sp in t.w.values():
                self._wait(SP, sp)
        nc = self.nc
        prog = self.prog
        with nc.Block() as block:
            @block.sync
            def _(e):
                for f in prog[SP]:
                    f(e)

            @block.tensor
            def _(e):
                for f in prog[PE]:
                    f(e)

            @block.scalar
            def _(e):
                for f in prog[ACT]:
                    f(e)

            @block.vector
            def _(e):
                for f in prog[DVE]:
                    f(e)

            @block.gpsimd
            def _(e):
                for f in prog[POOL]:
                    f(e)
        self.es.close()
        return nc


def lhsT_blocks(W, G=1):
    K, F = W.shape
    KC, NF = K // 128, F // 128
    a = W.reshape(KC, 128, NF, 128).transpose(2, 1, 0, 3).reshape(NF // G, G, 128, KC * 128)
    a = a.transpose(0, 2, 1, 3).reshape(NF // G, 128, G * KC * 128)
    return np.ascontiguousarray(a)


def fm(v):
    v = np.asarray(v)
    lead = v.shape[:-1]
    a = v.reshape(lead + (v.shape[-1] // 128, 128))
    a = np.moveaxis(a, -1, 0)
    return np.ascontiguousarray(a)


def toT(x):
    T, Dm = x.shape
    return np.ascontiguousarray(x.reshape(T, Dm // 128, 128).transpose(2, 1, 0))


def fromT(xT):
    p, c, T = xT.shape
    return np.ascontiguousarray(xT.transpose(2, 1, 0).reshape(T, c * p))


class Arena:
    def __init__(self, b, nf32):
        self.t = b.sb([128, nf32], F32, "arena")
        self.n = nf32
        self.off = 0

    def __enter__(self):
        self.off = 0
        return self

    def __exit__(self, *a):
        return False

    def get(self, shape, dt):
        nel = 1
        for s_ in shape[1:]:
            nel *= s_
        nf = (nel + 1) // 2 if dt == BF16 else nel
        nf = (nf + 7) // 8 * 8
        assert self.off + nf <= self.n, f"arena overflow {self.off}+{nf}>{self.n}"
        ap = self.t[:, self.off:self.off + nf]
        self.off += nf
        if dt == BF16:
            ap = ap.bitcast(BF16)[:, 0:nel]
        else:
            ap = ap[:, 0:nel]
        if len(shape) == 3:
            ap = ap.rearrange("p (a b) -> p a b", a=shape[1])
        return ap


class Ctx:
    pass


def tiles_of(total, start=0, step=512):
    out = []
    t = start
    while t < start + total:
        n = min(step, start + total - t)
        out.append((t, n))
        t += n
    return out


def setup_common(b, TT, arena_f32=25088):
    c = Ctx()
    c.b = b
    c.TT = TT
    c.xT = b.sb([128, 8, TT], F32, "xT")
    c.x_trk = Trk()
    c.ones = b.sb([128, 128], BF16, "ones")
    c.ones_trk = Trk()
    b.memset(DVE, c.ones[:], 1.0, [c.ones_trk])
    c.wslots = [(b.sb([128, 3072], BF16, "w"), Trk()) for _ in range(3)]
    c.wi = 0
    c.arena = Arena(b, arena_f32)
    c.modT = b.sb([128, 48, 2], F32, "modT")
    c.mod_trk = Trk()
    c.A = b.sb([128, 8, 2], F32, "A")
    c.A_trk = Trk()
    c.sq = b.sb([128, 8, 512], BF16, "sq")
    c.sq_trk = Trk()
    c.rstd = b.sb([128, 512], F32, "rstd")
    c.rstd_trk = Trk()
    c.tmp = [(b.sb([128, 512], F32, "tmp"), Trk()) for _ in range(2)]
    c.tmpi = 0
    c.eps = b.sb([128, 1], F32, "eps")
    c.eps_trk = Trk()
    b.memset(DVE, c.eps[:], EPS, [c.eps_trk])
    return c


def wslot(c):
    s = c.wslots[c.wi % len(c.wslots)]
    c.wi += 1
    return s


def load_mods(c, mod_d):
    b = c.b
    b.dma(SP, c.modT[:], mod_d, writes=[c.mod_trk])


def compute_mods(c, ada_w_d, ada_bT_d, ccT_d, es):
    b = c.b
    cc = b.sb([128, 8, 2], F32, "cc", es)
    cc_trk = Trk()
    ab = b.sb([128, 48, 2], F32, "ab", es)
    ab_trk = Trk()
    b.dma(SP, cc[:], ccT_d, writes=[cc_trk])
    b.dma(SP, ab[:], ada_bT_d, writes=[ab_trk])
    b.act(cc[:], cc[:], AF.Silu, [cc_trk], [cc_trk])
    aw = [(b.sb([128, 8, 256], F32, "aw", es), Trk()) for _ in range(2)]
    ps, ps_trk = b.bank(7)
    for g in range(24):
        t, tr = aw[g % 2]
        b.dma(SP, t[:], ada_w_d[:, g * 256:(g + 1) * 256].rearrange("(kc p) f -> p kc f", p=128), writes=[tr])
        for gi in range(2):
            f = g * 2 + gi
            for k in range(8):
                b.mm(ps[:, 2 * f:2 * f + 2], t[:, k, gi * 128:(gi + 1) * 128], cc[:, k, :], k == 0, k == 7,
                     [tr, cc_trk], [ps_trk], inc=(k == 7))
    b.tt(DVE, c.modT[:].rearrange("p j w -> p (j w)"), ps[:, 0:96], ab[:].rearrange("p j w -> p (j w)"), ALU.add,
         [ps_trk, ab_trk], [c.mod_trk])


def make_AB(c, ngT_d, which, es):
    b = c.b
    ng = b.sb([128, 8], F32, "ng", es)
    ng_trk = Trk()
    b.dma(SP, ng[:], ngT_d, writes=[ng_trk])
    base = which * 24 + 8
    for w in range(2):
        b.stt(DVE, c.A[:, :, w], c.modT[:, base:base + 8, w], 1.0, ng[:], ALU.add, ALU.mult,
              [c.mod_trk, ng_trk], [c.A_trk])


def norm_mod(c, outT, otrk, which, tiles, plain_g=None, cb=None):
    b = c.b
    for (t0, n, w, o) in tiles:
        for ch in range(8):
            b.act(c.sq[:, ch, 0:n], c.xT[:, ch, t0:t0 + n], AF.Square, [c.x_trk], [c.sq_trk])
        ps, ps_trk = b.nextbank()
        for ch in range(8):
            b.mm(ps[:, 0:n], c.ones[:], c.sq[:, ch, 0:n], ch == 0, ch == 7, [c.ones_trk, c.sq_trk], [ps_trk],
                 inc=(ch == 7))
        b.act(c.rstd[:, 0:n], ps[:, 0:n], AF.Sqrt, [ps_trk, c.eps_trk], [c.rstd_trk], bias=c.eps[:, 0:1],
              scale=1.0 / D)
        b.recip(c.rstd[:, 0:n], c.rstd[:, 0:n], [c.rstd_trk], [c.rstd_trk])
        for ch in range(8):
            tmp, tmp_trk = c.tmp[c.tmpi % 2]
            c.tmpi += 1
            b.tt(DVE, tmp[:, 0:n], c.xT[:, ch, t0:t0 + n], c.rstd[:, 0:n], ALU.mult,
                 [c.x_trk, c.rstd_trk], [tmp_trk])
            if plain_g is None:
                b.act(outT[:, ch, o:o + n], tmp[:, 0:n], AF.Identity, [tmp_trk, c.A_trk, c.mod_trk], [otrk],
                      bias=c.modT[:, which * 24 + ch, w:w + 1], scale=c.A[:, ch, w:w + 1])
            else:
                b.act(outT[:, ch, o:o + n], tmp[:, 0:n], AF.Copy, [tmp_trk, c.g_trk], [otrk],
                      scale=plain_g[:, ch:ch + 1])
        if cb is not None:
            cb(t0, n, o)


def linear_fm(c, wd, kc, G, ngroups, inT, in_trks, tiles, evac, banks=(0, 8)):
    b = c.b
    for g in range(ngroups):
        wt, wtrk = wslot(c)
        b.dma(POOL, wt[:, 0:G * kc * 128], wd[g], writes=[wtrk])
        for gi in range(G):
            f = g * G + gi
            for ti, (t0, n) in enumerate(tiles):
                itrk = in_trks[ti] if isinstance(in_trks, list) else in_trks
                ps, ps_trk = b.nextbank(*banks)
                for k in range(kc):
                    col = (gi * kc + k) * 128
                    b.mm(ps[:, 0:n], wt[:, col:col + 128], inT[:, k, t0:t0 + n], k == 0, k == kc - 1,
                         [wtrk, itrk], [ps_trk], inc=(k == kc - 1))
                evac(f, ti, t0, n, ps, ps_trk)


FFN_SUPERS = [[(0, 512, 0), (512, 512, 0), (1024, 128, 0)], [(1152, 512, 0), (1664, 384, 0), (2048, 256, 1)]]


def ffn(c, ng2_d, w_in_d, w_out_d, supers=None):
    b = c.b
    maxtok = 1152
    with c.arena as es:
        make_AB(c, ng2_d, 1, es)
        hT = b.sb([128, 8, maxtok], BF16, "hT2", es)
        h_trk = Trk()
        gT = b.sb([128, 22, maxtok], BF16, "gT", es)
        g_trk = Trk()
        sa = [(b.sb([128, maxtok], F32, "sa", es), Trk()) for _ in range(2)]
        for s in (supers or FFN_SUPERS):
            offs = []
            o = 0
            for (t0, n, _) in s:
                offs.append(o)
                o += n
            norm_mod(c, hT, h_trk, 1, [(s[i][0], s[i][1], s[i][2], offs[i]) for i in range(len(s))])
            tl = [(offs[i], s[i][1]) for i in range(len(s))]

            def evac1(f, ti, o, n, ps, ps_trk):
                j, isb = f // 2, f % 2
                st, st_trk = sa[j % 2]
                if not isb:
                    b.act(st[:, o:o + n], ps[:, 0:n], AF.Silu, [ps_trk], [st_trk])
                else:
                    b.tt(DVE, gT[:, j, o:o + n], ps[:, 0:n], st[:, o:o + n], ALU.mult, [ps_trk, st_trk], [g_trk])

            linear_fm(c, w_in_d, 8, 2, 22, hT, h_trk, tl, evac1)

            def evac2(f, ti, o, n, ps, ps_trk, s=s):
                t0, _, w = s[ti]
                b.stt(DVE, c.xT[:, f, t0:t0 + n], ps[:, 0:n], c.modT[:, 40 + f, w:w + 1], c.xT[:, f, t0:t0 + n],
                      ALU.mult, ALU.add, [ps_trk, c.mod_trk, c.x_trk], [c.x_trk])

            linear_fm(c, w_out_d, 22, 1, 8, gT, g_trk, tl, evac2)
        b.barrier()


def common_inputs(b):
    d = {}
    d["ada_w"] = b.din("ada_w", [D, 6 * D])
    d["ada_bT"] = b.din("ada_bT", [128, 48, 2])
    d["ccT"] = b.din("ccT", [128, 8, 2])
    d["ng1"] = b.din("ng1", [128, 8])
    d["ng2"] = b.din("ng2", [128, 8])
    d["ffn_w_in"] = b.din("ffn_w_in", [22, 128, 2 * 8 * 128])
    d["ffn_w_out"] = b.din("ffn_w_out", [8, 128, 22 * 128])
    return d


def build_layer1():
    b = Builder()
    TT = NT + NCX + 2
    xT_d = b.din("xT", [128, 8, TT])
    d = common_inputs(b)
    w_in_d = b.din("sc_w_in", [8, 128, 3 * 8 * 128])
    convT_d = b.din("convT", [128, 3, 8])
    hmask_d = b.din("hmask", [128, 2])
    w_out_d = b.din("sc_w_out", [8, 128, 8 * 128])
    xo_d, xo_trk = b.dout("xo", [128, 8, NT + NCX])

    c = setup_common(b, TT)
    for ch in range(8):
        b.dma(SP, c.xT[:, ch, :], xT_d[:, ch, :], writes=[c.x_trk])
    with c.arena as es:
        compute_mods(c, d["ada_w"], d["ada_bT"], d["ccT"], es)
        make_AB(c, d["ng1"], 0, es)
        b.barrier()
    with c.arena as es:
        hT = b.sb([128, 8, TT], BF16, "hT", es)
        h_trk = Trk()
        yT = b.sb([128, 8, NT + NCX], BF16, "yT", es)
        y_trk = Trk()
        bg = b.sb([128, TT], BF16, "bg", es)
        bg_trk = Trk()
        cg = b.sb([128, TT], BF16, "cg", es)
        cg_trk = Trk()
        zp = b.sb([128, NT + 2], F32, "zp", es)
        zc = b.sb([128, NCX + 2], F32, "zc", es)
        z_trk = Trk()
        zh = b.sb([128, 2], F32, "zh", es)
        zh_trk = Trk()
        conv = b.sb([128, 3, 8], F32, "convw", es)
        conv_trk = Trk()
        hm = b.sb([128, 2], F32, "hm", es)
        hm_trk = Trk()
        acc = [(b.sb([128, 512], F32, "acc", es), Trk()) for _ in range(2)]
        b.dma(SP, conv[:], convT_d, writes=[conv_trk])
        b.dma(SP, hm[:], hmask_d, writes=[hm_trk])
        b.memset(DVE, zc[:], 0.0, [z_trk])
        tl = tiles_of(NT) + tiles_of(NCX, NT) + [(NT + NCX, 2)]
        norm_mod(c, hT, h_trk, 0, [(t0, n, (1 if NT <= t0 < NT + NCX else 0), t0) for (t0, n) in tl])
        st = {"ai": 0}

        def evac_all(f, ti, t0, n, ps, ps_trk):
            ch, kind = f // 3, f % 3
            if kind == 0:
                b.op(ACT, lambda e, o=bg[:, t0:t0 + n], i=ps[:, 0:n]: e.copy(o, i), reads=[ps_trk], writes=[bg_trk])
                return
            if kind == 1:
                b.op(ACT, lambda e, o=cg[:, t0:t0 + n], i=ps[:, 0:n]: e.copy(o, i), reads=[ps_trk], writes=[cg_trk])
                return
            if t0 < NT:
                dst = zp[:, 1 + t0:1 + t0 + n]
            elif t0 < NT + NCX:
                dst = zc[:, 1 + t0 - NT:1 + t0 - NT + n]
            else:
                b.tt(DVE, zh[:, 0:2], ps[:, 0:2], cg[:, t0:t0 + 2], ALU.mult, [ps_trk, cg_trk], [zh_trk])
                b.tt(DVE, zp[:, 0:1], zh[:, 0:1], hm[:, 0:1], ALU.mult, [zh_trk, hm_trk], [z_trk])
                b.tt(DVE, zp[:, NT + 1:NT + 2], zh[:, 1:2], hm[:, 1:2], ALU.mult, [zh_trk, hm_trk], [z_trk])
                for (u0, m) in tiles_of(NT) + tiles_of(NCX, NT):
                    a, a_trk = acc[st["ai"] % 2]
                    st["ai"] += 1
                    if u0 < NT:
                        src, o = zp, u0
                    else:
                        src, o = zc, u0 - NT
                    b.ts(DVE, a[:, 0:m], src[:, o:o + m], conv[:, 0, ch:ch + 1], None, ALU.mult, None,
                         [z_trk, conv_trk], [a_trk])
                    b.stt(DVE, a[:, 0:m], src[:, o + 1:o + 1 + m], conv[:, 1, ch:ch + 1], a[:, 0:m],
                          ALU.mult, ALU.add, [z_trk, conv_trk, a_trk], [a_trk])
                    b.stt(DVE, a[:, 0:m], src[:, o + 2:o + 2 + m], conv[:, 2, ch:ch + 1], a[:, 0:m],
                          ALU.mult, ALU.add, [z_trk, conv_trk, a_trk], [a_trk])
                    b.tt(DVE, yT[:, ch, u0:u0 + m], a[:, 0:m], bg[:, u0:u0 + m], ALU.mult, [a_trk, bg_trk], [y_trk])
                return
            b.tt(DVE, dst, ps[:, 0:n], cg[:, t0:t0 + n], ALU.mult, [ps_trk, cg_trk], [z_trk])

        linear_fm(c, w_in_d, 8, 3, 8, hT, h_trk, tl, evac_all)
        tl2 = tiles_of(NT) + tiles_of(NCX, NT)

        def evac_out(f, ti, t0, n, ps, ps_trk):
            w = 0 if t0 < NT else 1
            b.stt(DVE, c.xT[:, f, t0:t0 + n], ps[:, 0:n], c.modT[:, 16 + f, w:w + 1], c.xT[:, f, t0:t0 + n],
                  ALU.mult, ALU.add, [ps_trk, c.mod_trk, c.x_trk], [c.x_trk])

        linear_fm(c, w_out_d, 8, 1, 8, yT, y_trk, tl2, evac_out)
        b.barrier()
    ffn(c, d["ng2"], d["ffn_w_in"], d["ffn_w_out"])
    for ch in range(8):
        b.dma(SP, xo_d[:, ch, :], c.xT[:, ch, 0:NT + NCX], reads=[c.x_trk], writes=[xo_trk], st=c.x_trk)
    return b.finish()


def layer1_inputs(xs, ctx, inp, i=1):
    sc_w_in = inp["sc_w_in"][0]
    cols = []
    for ch in range(8):
        cols += list(range(ch * 128, (ch + 1) * 128))
        cols += list(range(1024 + ch * 128, 1024 + (ch + 1) * 128))
        cols += list(range(2048 + ch * 128, 2048 + (ch + 1) * 128))
    w_in = lhsT_blocks(sc_w_in[:, cols], 3)
    common = {
        "ada_w": np.ascontiguousarray(inp["ada_w"][i]),
        "ada_bT": np.ascontiguousarray(np.repeat(fm(inp["ada_b"][i])[:, :, None], 2, axis=2)),
        "ccT": np.ascontiguousarray(np.stack([fm(inp["c"][0]), fm(inp["c_ctx"])], axis=-1)),
        "ng1": fm(inp["norm1_g"][i]),
        "ng2": fm(inp["norm2_g"][i]),
        "sc_w_in": w_in,
        "convT": fm(inp["sc_conv_w"][0]),
        "sc_w_out": lhsT_blocks(inp["sc_w_out"][0], 1),
        "ffn_w_in": lhsT_blocks(ffn_in_cols(inp["ffn_w_in"][i]), 2),
        "ffn_w_out": lhsT_blocks(inp["ffn_w_out"][i], 1),
    }
    maps = []
    z = np.zeros((1, D), np.float32)
    for cid in range(NCORES):
        lo, hi = cid * NT, (cid + 1) * NT
        left = xs[lo - 1:lo] if cid > 0 else z
        right = xs[hi:hi + 1] if cid < NCORES - 1 else z
        xt = np.concatenate([xs[lo:hi], ctx, left, right], axis=0)
        m = dict(common)
        m["xT"] = toT(xt)
        hm = np.ones((128, 2), np.float32)
        if cid == 0:
            hm[:, 0] = 0
        if cid == NCORES - 1:
            hm[:, 1] = 0
        m["hmask"] = hm
        maps.append(m)
    return maps


def ffn_in_cols(w):
    cols = []
    for j in range(22):
        cols += list(range(j * 128, (j + 1) * 128))
        cols += list(range(DFF + j * 128, DFF + (j + 1) * 128))
    return w[:, cols]


def run_layer1(xs, ctx, inp):
    nc = build_layer1()
    maps = layer1_inputs(xs, ctx, inp)
    res = run_bass_kernel_spmd(nc, maps, core_ids=list(range(NCORES)))
    xo = [fromT(r["xo"]) for r in res.results]
    x_new = np.concatenate([o[:NT] for o in xo], axis=0)
    ctx_new = xo[0][NT:]
    return x_new, ctx_new


def rope_tables_T(pos):
    pos = np.asarray(pos)
    inv = (1.0 / (np.float32(10000.0) ** (np.arange(0, 32, 2, dtype=np.float32) / np.float32(32.0)))).astype(np.float32)
    pp = np.maximum(pos, 0)
    row = (pp // 64).astype(np.float32)
    col = (pp % 64).astype(np.float32)
    ang = np.concatenate([row[:, None] * inv[None, :], col[:, None] * inv[None, :]], axis=-1).astype(np.float32)
    cos = np.cos(ang).astype(np.float32)
    sin = np.sin(ang).astype(np.float32)
    cos[pos < 0] = 1.0
    sin[pos < 0] = 0.0
    C = np.zeros((128, len(pos)), np.float32)
    S = np.zeros((128, len(pos)), np.float32)
    for hh in range(2):
        for a in range(2):
            for half in range(2):
                p0 = hh * 64 + a * 32 + half * 16
                C[p0:p0 + 16] = cos[:, a * 16:(a + 1) * 16].T
                S[p0:p0 + 16] = (sin[:, a * 16:(a + 1) * 16].T) * (-1.0 if half == 0 else 1.0)
    return C, S


def swap_cols(W):
    K, F = W.shape
    a = W.reshape(K, F // 64, 2, 2, 16)
    return np.ascontiguousarray(a[:, :, :, ::-1, :].reshape(K, F))


def kcol2(t0):
    if t0 < NT:
        return 128 + t0
    if t0 < NT + NCX:
        return 2304 + (t0 - NT)
    if t0 < NT + NCX + 128:
        return 0
    return 2176


def build_layer2():
    b = Builder()
    TT = NT + NCX + 256
    NQ = NT + NCX
    xT_d = b.din("xT", [128, 8, TT])
    d = common_inputs(b)
    wqk_d = b.din("wqk", [12, 128, 2048])
    wv_d = b.din("wv", [128, 8 * 256])
    ropeC_d = b.din("ropeC", [128, TT])
    ropeS_d = b.din("ropeS", [128, TT])
    masks_d = b.din("masks", [128, 8, 512])
    sink_d = b.din("sinks", [128, 16])
    wo_d = b.din("wo", [8, 128, 1024])
    xo_d, xo_trk = b.dout("xo", [128, 8, NQ])
    q_scr = b.dscr("q_scr", [8, 128, NQ])
    k_scr = b.dscr("k_scr", [4, 128, TT])
    v_scr = b.dscr("v_scr", [20, 128, 512])
    scr_trk = Trk()

    c = setup_common(b, TT, 23000)
    for ch in range(8):
        b.dma(SP, c.xT[:, ch, :], xT_d[:, ch, :], writes=[c.x_trk])
    with c.arena as es:
        compute_mods(c, d["ada_w"], d["ada_bT"], d["ccT"], es)
        make_AB(c, d["ng1"], 0, es)
        b.barrier()
    tl_q = tiles_of(NT) + tiles_of(NCX, NT)
    tl_all = tl_q + [(NQ, 128), (NQ + 128, 128)]
    with c.arena as es:
        hT = b.sb([128, 8, TT], BF16, "hT", es)
        h_trk = Trk()
        C = b.sb([128, TT], F32, "C", es)
        S = b.sb([128, TT], F32, "S", es)
        cs_trk = Trk()
        rt = b.sb([128, TT], F32, "rt", es)
        rt_trk = Trk()
        stg = [(b.sb([128, 512], BF16, "stg", es), Trk()) for _ in range(3)]
        vst = [(b.sb([128, 4, 128], BF16, "vst", es), Trk()) for _ in range(2)]
        st = {"i": 0, "v": 0}
        b.dma(SP, C[:], ropeC_d, writes=[cs_trk])
        b.dma(SP, S[:], ropeS_d, writes=[cs_trk])
        norm_mod(c, hT, h_trk, 0, [(t0, n, (1 if NT <= t0 < NQ else 0), t0) for (t0, n) in tl_all])

        def mk_evac(isk):
            def evac(f, ti, t0, n, ps, ps_trk):
                grp, sw = f // 2, f % 2
                if not sw:
                    b.tt(DVE, rt[:, t0:t0 + n], ps[:, 0:n], C[:, t0:t0 + n], ALU.mult, [ps_trk, cs_trk], [rt_trk])
                    return
                tmp, tmp_trk = c.tmp[c.tmpi % 2]
                c.tmpi += 1
                b.tt(DVE, tmp[:, 0:n], ps[:, 0:n], S[:, t0:t0 + n], ALU.mult, [ps_trk, cs_trk], [tmp_trk])
                sg, sg_trk = stg[st["i"] % 3]
                st["i"] += 1
                b.tt(DVE, sg[:, 0:n], tmp[:, 0:n], rt[:, t0:t0 + n], ALU.add, [tmp_trk, rt_trk], [sg_trk])
                if not isk:
                    b.dma(SP, q_scr[grp, :, t0:t0 + n], sg[:, 0:n], reads=[sg_trk], writes=[scr_trk], st=sg_trk)
                else:
                    kc0 = kcol2(t0)
                    b.dma(SP, k_scr[grp, :, kc0:kc0 + n], sg[:, 0:n], reads=[sg_trk], writes=[scr_trk], st=sg_trk)
            return evac

        linear_fm(c, wqk_d[0:8], 8, 2, 8, hT, h_trk, tl_q, mk_evac(False))
        linear_fm(c, wqk_d[8:12], 8, 2, 4, hT, h_trk, tl_all, mk_evac(True))
        wt, wtrk = wslot(c)
        b.dma(POOL, wt[:, 0:2048], wv_d, writes=[wtrk])
        for (t0, n) in tl_all:
            for u in range(n // 128):
                tk = t0 + u * 128
                kt = kcol2(t0) // 128 + u
                ps, ps_trk = b.nextbank()
                for k in range(8):
                    b.mm(ps[:, 0:256], hT[:, k, tk:tk + 128], wt[:, k * 256:(k + 1) * 256], k == 0, k == 7,
                         [h_trk, wtrk], [ps_trk], inc=(k == 7))
                vs, vs_trk = vst[st["v"] % 2]
                st["v"] += 1
                src = ps[:, 0:256].rearrange("p (g e) -> p g e", g=4)
                b.copy(DVE, vs[:, :, 0:64], src, [ps_trk], [vs_trk])
                b.copy(DVE, vs[:, :, 64:128], src, [ps_trk], [vs_trk])
                b.dma(SP, v_scr[kt], vs[:].rearrange("p g e -> p (g e)"), reads=[vs_trk], writes=[scr_trk], st=vs_trk)
        b.barrier()
    with c.arena as es:
        oT = b.sb([128, 8, NQ], BF16, "oT", es)
        o_trk = Trk()
        masks = b.sb([128, 8, 512], BF16, "masks", es)
        m_trk = Trk()
        sink = b.sb([128, 16], F32, "sink", es)
        sk_trk = Trk()
        kTr = [(b.sb([128, TT], BF16, "kT", es), Trk()) for _ in range(2)]
        vTr = [(b.sb([128, 20, 128], BF16, "vT", es), Trk()) for _ in range(2)]
        qTr = [(b.sb([128, NQ], BF16, "qT", es), Trk()) for _ in range(2)]
        pTr = [(b.sb([128, 512], BF16, "pT", es), Trk()) for _ in range(3)]
        rdr = [(b.sb([128, 512], F32, "rd", es), Trk()) for _ in range(2)]
        b.dma(POOL, masks[:], masks_d, writes=[m_trk])
        b.dma(SP, sink[:], sink_d, writes=[sk_trk])
        b.act(sink[:], sink[:], AF.Exp, [sk_trk], [sk_trk])
        cnt = {"p": 0, "s": 0, "u": 0, "q": 0}
        for g in range(4):
            kT, k_trk = kTr[g % 2]
            vT, v_trk = vTr[g % 2]
            b.dma(SP, kT[:], k_scr[g], reads=[scr_trk], writes=[k_trk])
            for t4 in range(0, 20, 4):
                b.dma(SP, vT[:, t4:t4 + 4, :], v_scr[t4:t4 + 4, :, g * 128:(g + 1) * 128].rearrange("t p e -> p t e"),
                      reads=[scr_trk], writes=[v_trk])
            for r in range(4):
                h = g * 4 + r
                qc, half = h // 2, h % 2
                if r % 2 == 0:
                    qT, q_trk = qTr[cnt["q"] % 2]
                    cnt["q"] += 1
                    b.dma(SP, qT[:], q_scr[qc], reads=[scr_trk], writes=[q_trk])
                rows = slice(half * 64, half * 64 + 64)
                units = []
                for qt in range(4):
                    keys = [(18, 0, 512, None), (19, 0, 512, None)]
                    for j in range(6):
                        mi = j
                        if qt == 0 and j == 0:
                            mi = 6
                        if qt == 3 and j == 5:
                            mi = 7
                        keys.append((4 * qt + j, max(0, j - 2) * 128, (min(3, j) + 1) * 128, mi))
                    units.append((qt * 512, 512, keys))
                units.append((NT, NCX, [(18, 0, 256, None), (19, 0, 256, None)]))
                for (q0, nq, keys) in units:
                    ob, ob_trk = b.bank(4 + cnt["u"] % 2)
                    db, db_trk = b.bank(6 + cnt["u"] % 2)
                    cnt["u"] += 1
                    for idx, (kt, qlo, qhi, mi) in enumerate(keys):
                        nn = qhi - qlo
                        sb_, sb_trk = b.bank(cnt["s"] % 4)
                        cnt["s"] += 1
                        b.mm(sb_[:, 0:nn], kT[rows, kt * 128:(kt + 1) * 128], qT[rows, q0 + qlo:q0 + qhi], True, True,
                             [k_trk, q_trk], [sb_trk], inc=True)
                        pT, p_trk = pTr[cnt["p"] % 3]
                        cnt["p"] += 1
                        b.act(pT[:, 0:nn], sb_[:, 0:nn], AF.Exp, [sb_trk], [p_trk], scale=0.125)
                        if mi is not None:
                            b.tt(DVE, pT[:, 0:nn], pT[:, 0:nn], masks[:, mi, qlo:qhi], ALU.mult, [p_trk, m_trk], [p_trk])
                        last = idx == len(keys) - 1
                        b.mm(ob[:, qlo:qhi], vT[:, kt, :], pT[:, 0:nn], idx == 0, last, [v_trk, p_trk], [ob_trk],
                             inc=False)
                        b.mm(db[:, qlo:qhi], c.ones[:], pT[:, 0:nn], idx == 0, last, [c.ones_trk, p_trk], [db_trk],
                             inc=True)
                    rd, rd_trk = rdr[cnt["u"] % 2]
                    b.ts(DVE, rd[:, 0:nq], db[:, 0:nq], sink[:, h:h + 1], None, ALU.add, None, [db_trk, sk_trk], [rd_trk])
                    b.recip(rd[:, 0:nq], rd[:, 0:nq], [rd_trk], [rd_trk])
                    b.tt(DVE, oT[rows, qc, q0:q0 + nq], ob[rows, 0:nq], rd[rows, 0:nq], ALU.mult, [ob_trk, rd_trk],
                         [o_trk])

        def evac_out(f, ti, t0, n, ps, ps_trk):
            w = 0 if t0 < NT else 1
            b.stt(DVE, c.xT[:, f, t0:t0 + n], ps[:, 0:n], c.modT[:, 16 + f, w:w + 1], c.xT[:, f, t0:t0 + n],
                  ALU.mult, ALU.add, [ps_trk, c.mod_trk, c.x_trk], [c.x_trk])

        linear_fm(c, wo_d, 8, 1, 8, oT, o_trk, tl_q, evac_out, banks=(0, 4))
        b.barrier()
    ffn(c, d["ng2"], d["ffn_w_in"], d["ffn_w_out"])
    for ch in range(8):
        b.dma(SP, xo_d[:, ch, :], c.xT[:, ch, 0:NQ], reads=[c.x_trk], writes=[xo_trk], st=c.x_trk)
    return b.finish()


def common_host(inp, i):
    return {
        "ada_w": np.ascontiguousarray(inp["ada_w"][i]),
        "ada_bT": np.ascontiguousarray(np.repeat(fm(inp["ada_b"][i])[:, :, None], 2, axis=2)),
        "ccT": np.ascontiguousarray(np.stack([fm(inp["c"][0]), fm(inp["c_ctx"])], axis=-1)),
        "ng1": fm(inp["norm1_g"][i]),
        "ng2": fm(inp["norm2_g"][i]),
        "ffn_w_in": lhsT_blocks(ffn_in_cols(inp["ffn_w_in"][i]), 2),
        "ffn_w_out": lhsT_blocks(inp["ffn_w_out"][i], 1),
    }


def layer2_inputs(xs, ctx, inp, i=2):
    W = inp["wa_wqkv"][0]
    Wq, Wk, Wv = W[:, 0:1024], W[:, 1024:1280], W[:, 1280:1536]
    Wqs, Wks = swap_cols(Wq), swap_cols(Wk)
    blocks = []
    for qc in range(8):
        blocks += [Wq[:, qc * 128:(qc + 1) * 128], Wqs[:, qc * 128:(qc + 1) * 128]]
    for g in range(4):
        kg, kgs = Wk[:, g * 64:(g + 1) * 64], Wks[:, g * 64:(g + 1) * 64]
        blocks += [np.concatenate([kg, kg], axis=1), np.concatenate([kgs, kgs], axis=1)]
    wqk = lhsT_blocks(np.concatenate(blocks, axis=1), 2)
    common = common_host(inp, i)
    common.update({
        "wqk": wqk,
        "wv": np.ascontiguousarray(Wv.reshape(8, 128, 256).transpose(1, 0, 2).reshape(128, 2048)),
        "sinks": np.ascontiguousarray(np.broadcast_to(inp["wa_sinks"][0][None, :], (128, 16))),
        "wo": lhsT_blocks(inp["wa_wo"][0], 1),
    })
    k = np.arange(128)[:, None]
    q = np.arange(512)[None, :]
    base_masks = [(np.abs((j - 1) * 128 + k - q) <= 128).astype(np.float32) for j in range(6)]
    maps = []
    z = np.zeros((128, D), np.float32)
    for cid in range(NCORES):
        lo, hi = cid * NT, (cid + 1) * NT
        left = xs[lo - 128:lo] if cid > 0 else z
        right = xs[hi:hi + 128] if cid < NCORES - 1 else z
        xt = np.concatenate([xs[lo:hi], ctx, left, right], axis=0)
        pos = np.concatenate([np.arange(lo, hi), -np.ones(NCX, np.int64), np.arange(lo - 128, lo), np.arange(hi, hi + 128)])
        pos = np.where((pos >= SEQ), 0, pos)
        if cid == 0:
            pos[NT + NCX:NT + NCX + 128] = 0
        Ct, St = rope_tables_T(pos)
        m = dict(common)
        m["xT"] = toT(xt)
        m["ropeC"], m["ropeS"] = Ct, St
        mk = base_masks + [base_masks[0] * (0.0 if cid == 0 else 1.0), base_masks[5] * (0.0 if cid == NCORES - 1 else 1.0)]
        m["masks"] = np.ascontiguousarray(np.stack(mk, axis=1))
        maps.append(m)
    return maps


def run_layer2(xs, ctx, inp):
    nc = build_layer2()
    maps = layer2_inputs(xs, ctx, inp)
    res = run_bass_kernel_spmd(nc, maps, core_ids=list(range(NCORES)))
    xo = [fromT(r["xo"]) for r in res.results]
    return np.concatenate([o[:NT] for o in xo], axis=0), xo[0][NT:]


LAM_INIT0 = 0.8 - 0.6 * math.exp(-0.3 * 0)
NKT = (SEQ + NCX) // 128
KCH = 13


def build_A0():
    b = Builder()
    TT = NT + NCX
    xT_d = b.din("xT", [128, 8, TT])
    d = common_inputs(b)
    wqk_d = b.din("wqk", [16, 128, 2048])
    wv_d = b.din("wv", [128, 8 * 1024])
    ropeC_d = b.din("ropeC", [128, TT])
    ropeS_d = b.din("ropeS", [128, TT])
    q_o, q_trk = b.dout("qT_o", [8, 128, TT], BF16)
    k_o, k_trk = b.dout("kT_o", [8, 128, TT], BF16)
    v_o, v_trk = b.dout("v_o", [18, 128, 1024], BF16)
    mod_o, mod_trk = b.dout("modT_o", [128, 96])
    c = setup_common(b, TT)
    for ch in range(8):
        b.dma(SP, c.xT[:, ch, :], xT_d[:, ch, :], writes=[c.x_trk])
    with c.arena as es:
        compute_mods(c, d["ada_w"], d["ada_bT"], d["ccT"], es)
        make_AB(c, d["ng1"], 0, es)
        b.dma(SP, mod_o, c.modT[:].rearrange("p j w -> p (j w)"), reads=[c.mod_trk], writes=[mod_trk], st=c.mod_trk)
        b.barrier()
    tl = tiles_of(NT) + tiles_of(NCX, NT)
    with c.arena as es:
        hT = b.sb([128, 8, TT], BF16, "hT", es)
        h_trk = Trk()
        C = b.sb([128, TT], F32, "C", es)
        S = b.sb([128, TT], F32, "S", es)
        cs_trk = Trk()
        rt = b.sb([128, TT], F32, "rt", es)
        rt_trk = Trk()
        wv = b.sb([128, 8 * 1024], BF16, "wv", es)
        wv_trk = Trk()
        stg = [(b.sb([128, 512], BF16, "stg", es), Trk()) for _ in range(3)]
        vst = [(b.sb([128, 1024], BF16, "vst", es), Trk()) for _ in range(2)]
        st = {"i": 0, "v": 0}
        b.dma(SP, C[:], ropeC_d, writes=[cs_trk])
        b.dma(SP, S[:], ropeS_d, writes=[cs_trk])
        for k in range(8):
            b.dma(POOL, wv[:, k * 1024:(k + 1) * 1024], wv_d[:, k * 1024:(k + 1) * 1024], writes=[wv_trk])
        norm_mod(c, hT, h_trk, 0, [(t0, n, (1 if t0 >= NT else 0), t0) for (t0, n) in tl])

        def evac(f, ti, t0, n, ps, ps_trk):
            grp, sw = f // 2, f % 2
            if not sw:
                b.tt(DVE, rt[:, t0:t0 + n], ps[:, 0:n], C[:, t0:t0 + n], ALU.mult, [ps_trk, cs_trk], [rt_trk])
                return
            tmp, tmp_trk = c.tmp[c.tmpi % 2]
            c.tmpi += 1
            b.tt(DVE, tmp[:, 0:n], ps[:, 0:n], S[:, t0:t0 + n], ALU.mult, [ps_trk, cs_trk], [tmp_trk])
            sg, sg_trk = stg[st["i"] % 3]
            st["i"] += 1
            b.tt(DVE, sg[:, 0:n], tmp[:, 0:n], rt[:, t0:t0 + n], ALU.add, [tmp_trk, rt_trk], [sg_trk])
            if grp < 8:
                b.dma(SP, q_o[grp, :, t0:t0 + n], sg[:, 0:n], reads=[sg_trk], writes=[q_trk], st=sg_trk)
            else:
                b.dma(SP, k_o[grp - 8, :, t0:t0 + n], sg[:, 0:n], reads=[sg_trk], writes=[k_trk], st=sg_trk)

        linear_fm(c, wqk_d, 8, 2, 16, hT, h_trk, tl, evac)
        for kt in range(18):
            tk = kt * 128
            vs, vs_trk = vst[st["v"] % 2]
            st["v"] += 1
            for hf in range(2):
                ps, ps_trk = b.nextbank()
                for k in range(8):
                    b.mm(ps[:, 0:512], hT[:, k, tk:tk + 128], wv[:, k * 1024 + hf * 512:k * 1024 + hf * 512 + 512],
                         k == 0, k == 7, [h_trk, wv_trk], [ps_trk], inc=(k == 7))
                if hf == 0:
                    b.copy(DVE, vs[:, 0:512], ps[:, 0:512], [ps_trk], [vs_trk])
                else:
                    b.op(ACT, lambda e, o=vs[:, 512:1024], i=ps[:, 0:512]: e.copy(o, i), reads=[ps_trk], writes=[vs_trk])
            b.dma(SP, v_o[kt], vs[:], reads=[vs_trk], writes=[v_trk], st=vs_trk)
        b.barrier()
    return b.finish()


def build_B0():
    b = Builder()
    TT = NT + NCX
    xT_d = b.din("xT", [128, 8, TT])
    d = common_inputs(b)
    mod_d = b.din("modT", [128, 96])
    q_d = b.din("qT", [8, 128, TT], BF16)
    k_d = b.din("kT_all", [8, 128, NKT * 128], BF16)
    v_d = b.din("v_all", [NKT, 128, 1024], BF16)
    lam_d = b.din("lamb", [128, 4, 64])
    gs_d = b.din("gsub", [128, 1])
    wo_d = b.din("wo", [8, 128, 1024])
    xo_d, xo_trk = b.dout("xo", [128, 8, TT])
    c = setup_common(b, TT)
    for ch in range(8):
        b.dma(SP, c.xT[:, ch, :], xT_d[:, ch, :], writes=[c.x_trk])
    b.dma(SP, c.modT[:].rearrange("p j w -> p (j w)"), mod_d, writes=[c.mod_trk])
    tl = tiles_of(NT) + tiles_of(NCX, NT)
    with c.arena as es:
        oT = b.sb([128, 8, TT], BF16, "oT", es)
        o_trk = Trk()
        qTr = [(b.sb([128, TT], BF16, "qT", es), Trk()) for _ in range(2)]
        kcr = [(b.sb([128, KCH * 128], BF16, "kch", es), Trk()) for _ in range(3)]
        vcr = [(b.sb([128, KCH, 128], BF16, "vch", es), Trk()) for _ in range(3)]
        pTr = [(b.sb([128, 2, 512], BF16, "pT", es), Trk()) for _ in range(3)]
        t1 = b.sb([128, 512], F32, "t1", es)
        t2 = b.sb([128, 512], F32, "t2", es)
        r1 = b.sb([128, 512], F32, "r1", es)
        r2 = b.sb([128, 512], F32, "r2", es)
        f_trk = Trk()
        sqb = b.sb([128, 512], BF16, "sqb", es)
        sq_trk = Trk()
        lamb = b.sb([128, 4, 64], F32, "lamb", es)
        lam_trk = Trk()
        lj = b.sb([128, 2, 64], F32, "lj", es)
        ls = b.sb([128, 4], F32, "ls", es)
        gs = b.sb([128, 1], F32, "gs", es)
        gs_trk = Trk()
        b.dma(SP, lamb[:], lam_d, writes=[lam_trk])
        b.dma(SP, gs[:], gs_d, writes=[gs_trk])
        b.memset(DVE, ls[:], 0.0, [lam_trk])
        b.tt(DVE, lj[:, 0, :], lamb[:, 0, :], lamb[:, 1, :], ALU.mult, [lam_trk], [lam_trk])
        b.tt(DVE, lj[:, 1, :], lamb[:, 2, :], lamb[:, 3, :], ALU.mult, [lam_trk], [lam_trk])
        b.act(lj[:, 0, :], lj[:, 0, :], AF.Copy, [lam_trk], [lam_trk], accum_out=ls[:, 0:1])
        b.act(lj[:, 1, :], lj[:, 1, :], AF.Copy, [lam_trk], [lam_trk], accum_out=ls[:, 1:2])
        b.act(ls[:, 0:2], ls[:, 0:2], AF.Exp, [lam_trk], [lam_trk])
        b.tt(DVE, ls[:, 2:3], ls[:, 1:2], ls[:, 0:1], ALU.subtract, [lam_trk], [lam_trk])
        b.ts(DVE, ls[:, 2:3], ls[:, 2:3], -LAM_INIT0, None, ALU.add, None, [lam_trk], [lam_trk])
        b.ts(DVE, gs[:], gs[:], 1.0 - LAM_INIT0, None, ALU.mult, None, [gs_trk], [gs_trk])
        cnt = {"p": 0, "s": 0, "k": 0, "q": 0}
        ob1, ob1_trk = b.bank(4)
        ob2, ob2_trk = b.bank(5)
        db1, db1_trk = b.bank(6)
        db2, db2_trk = b.bank(7)
        for h in range(8):
            qT, q_trk = qTr[h % 2]
            b.dma(SP, qT[:], q_d[h], writes=[q_trk])
            units = [(qt * 512, 512, NKT) for qt in range(4)] + [(NT, NCX, 2)]
            for (q0, nq, nkt) in units:
                nch = (nkt + KCH - 1) // KCH
                for ch in range(nch):
                    kt0 = ch * KCH
                    nk = min(KCH, nkt - kt0)
                    kc_, kc_trk = kcr[cnt["k"] % 3]
                    vc_, vc_trk = vcr[cnt["k"] % 3]
                    cnt["k"] += 1
                    b.dma(SP, kc_[:, 0:nk * 128], k_d[h, :, kt0 * 128:(kt0 + nk) * 128], writes=[kc_trk])
                    for v0 in range(0, nk, 7):
                        v1 = min(nk, v0 + 7)
                        b.dma(SP, vc_[:, v0:v1, :],
                              v_d[kt0 + v0:kt0 + v1, :, h * 128:(h + 1) * 128].rearrange("t p e -> p t e"),
                              writes=[vc_trk])
                    for u in range(nk):
                        kt = kt0 + u
                        si = cnt["s"] % 2
                        cnt["s"] += 1
                        spair = b.pst[si]
                        s_trks = [b.bank_trk[2 * si], b.bank_trk[2 * si + 1]]
                        b.mm(spair[:, 0:nq], kc_[0:64, u * 128:(u + 1) * 128], qT[0:64, q0:q0 + nq], True, True,
                             [kc_trk, q_trk], s_trks, inc=False)
                        b.mm(spair[:, 512:512 + nq], kc_[64:128, u * 128:(u + 1) * 128], qT[64:128, q0:q0 + nq], True,
                             True, [kc_trk, q_trk], s_trks, inc=True)
                        pT, p_trk = pTr[cnt["p"] % 3]
                        cnt["p"] += 1
                        b.act(pT[:, :, 0:nq], spair[:].rearrange("p (m q) -> p m q", m=2)[:, :, 0:nq], AF.Exp, s_trks,
                              [p_trk], scale=0.125)
                        first, last = kt == 0, kt == nkt - 1
                        b.mm(ob1[:, 0:nq], vc_[:, u, :], pT[:, 0, 0:nq], first, last, [vc_trk, p_trk], [ob1_trk], inc=False)
                        b.mm(ob2[:, 0:nq], vc_[:, u, :], pT[:, 1, 0:nq], first, last, [vc_trk, p_trk], [ob2_trk], inc=False)
                        b.mm(db1[:, 0:nq], c.ones[:], pT[:, 0, 0:nq], first, last, [c.ones_trk, p_trk], [db1_trk], inc=False)
                        b.mm(db2[:, 0:nq], c.ones[:], pT[:, 1, 0:nq], first, last, [c.ones_trk, p_trk], [db2_trk], inc=True)
                b.recip(r1[:, 0:nq], db1[:, 0:nq], [db1_trk], [f_trk])
                b.recip(r2[:, 0:nq], db2[:, 0:nq], [db2_trk], [f_trk])
                b.tt(DVE, t1[:, 0:nq], ob1[:, 0:nq], r1[:, 0:nq], ALU.mult, [ob1_trk, f_trk], [f_trk])
                b.tt(DVE, t2[:, 0:nq], ob2[:, 0:nq], r2[:, 0:nq], ALU.mult, [ob2_trk, f_trk], [f_trk])
                b.stt(DVE, t1[:, 0:nq], t2[:, 0:nq], ls[:, 2:3], t1[:, 0:nq], ALU.mult, ALU.add, [f_trk, lam_trk], [f_trk])
                b.act(sqb[:, 0:nq], t1[:, 0:nq], AF.Square, [f_trk], [sq_trk])
                si = cnt["s"] % 2
                cnt["s"] += 1
                ssb = b.pst[si][:, 0:512]
                ss_trks = [b.bank_trk[2 * si], b.bank_trk[2 * si + 1]]
                b.mm(ssb[:, 0:nq], c.ones[:], sqb[:, 0:nq], True, True, [c.ones_trk, sq_trk], ss_trks, inc=True)
                b.act(r1[:, 0:nq], ssb[:, 0:nq], AF.Sqrt, ss_trks + [c.eps_trk], [f_trk], bias=c.eps[:, 0:1], scale=1.0 / 128)
                b.recip(r1[:, 0:nq], r1[:, 0:nq], [f_trk], [f_trk])
                b.tt(DVE, t1[:, 0:nq], t1[:, 0:nq], r1[:, 0:nq], ALU.mult, [f_trk], [f_trk])
                b.act(oT[:, h, q0:q0 + nq], t1[:, 0:nq], AF.Copy, [f_trk, gs_trk], [o_trk], scale=gs[:, 0:1])

        def evac_out(f, ti, t0, n, ps, ps_trk):
            w = 0 if t0 < NT else 1
            b.stt(DVE, c.xT[:, f, t0:t0 + n], ps[:, 0:n], c.modT[:, 16 + f, w:w + 1], c.xT[:, f, t0:t0 + n],
                  ALU.mult, ALU.add, [ps_trk, c.mod_trk, c.x_trk], [c.x_trk])

        linear_fm(c, wo_d, 8, 1, 8, oT, o_trk, tl, evac_out, banks=(0, 4))
        b.barrier()
    ffn(c, d["ng2"], d["ffn_w_in"], d["ffn_w_out"])
    for ch in range(8):
        b.dma(SP, xo_d[:, ch, :], c.xT[:, ch, :], reads=[c.x_trk], writes=[xo_trk], st=c.x_trk)
    return b.finish()


def run_layer0(xs, ctx, inp):
    W = inp["da_wqkv"][0]
    Wq, Wk, Wv = W[:, 0:1024], W[:, 1024:2048], W[:, 2048:3072]
    Wqs, Wks = swap_cols(Wq), swap_cols(Wk)
    blocks = []
    for hh in range(8):
        blocks += [Wq[:, hh * 128:(hh + 1) * 128], Wqs[:, hh * 128:(hh + 1) * 128]]
    for hh in range(8):
        blocks += [Wk[:, hh * 128:(hh + 1) * 128], Wks[:, hh * 128:(hh + 1) * 128]]
    common = common_host(inp, 0)
    a_common = dict(common)
    a_common.update({
        "wqk": lhsT_blocks(np.concatenate(blocks, axis=1), 2),
        "wv": np.ascontiguousarray(Wv.reshape(8, 128, 1024).transpose(1, 0, 2).reshape(128, 8192)),
    })
    maps = []
    for cid in range(NCORES):
        lo, hi = cid * NT, (cid + 1) * NT
        pos = np.concatenate([np.arange(lo, hi), -np.ones(NCX, np.int64)])
        Ct, St = rope_tables_T(pos)
        m = dict(a_common)
        m["xT"] = toT(np.concatenate([xs[lo:hi], ctx], axis=0))
        m["ropeC"], m["ropeS"] = Ct, St
        maps.append(m)
    resA = run_bass_kernel_spmd(build_A0(), maps, core_ids=list(range(NCORES))).results
    kT_all = np.concatenate([resA[0]["kT_o"][:, :, NT:]] + [r["kT_o"][:, :, 0:NT] for r in resA], axis=2)
    v_all = np.concatenate([resA[0]["v_o"][16:18]] + [r["v_o"][0:16] for r in resA], axis=0)
    kT_all = np.ascontiguousarray(kT_all)
    v_all = np.ascontiguousarray(v_all)
    b_common = dict(common)
    b_common.update({
        "kT_all": kT_all, "v_all": v_all,
        "lamb": np.ascontiguousarray(np.broadcast_to(inp["da_lambda"][0][None], (128, 4, 64))),
        "gsub": np.ascontiguousarray(inp["da_subln_g"][0].reshape(128, 1)),
        "wo": lhsT_blocks(inp["da_wo"][0], 1),
    })
    mapsB = []
    for cid in range(NCORES):
        m = dict(b_common)
        m["xT"] = maps[cid]["xT"]
        m["modT"] = resA[cid]["modT_o"]
        m["qT"] = resA[cid]["qT_o"]
        mapsB.append(m)
    resB = run_bass_kernel_spmd(build_B0(), mapsB, core_ids=list(range(NCORES))).results
    xo = [fromT(r["xo"]) for r in resB]
    return np.concatenate([o[:NT] for o in xo], axis=0), xo[0][NT:]


def hg_consts(c, es, b):
    k = Ctx()
    k.trif = b.sb([128, 2, 128], F32, "trif", es)
    k.onesf = b.sb([128, 128], F32, "onesf", es)
    k.maskb = b.sb([128, 2, 128], BF16, "maskb", es)
    k.ident = b.sb([128, 128], BF16, "ident", es)
    k.trk = Trk()
    b.dma(SP, k.trif[:], c.tri_d, writes=[k.trk])
    b.dma(POOL, k.maskb[:], c.tri_d, writes=[k.trk])
    b.dma(POOL, k.ident[:], c.ident_d, writes=[k.trk])
    b.memset(DVE, k.onesf[:], 1.0, [k.trk])
    return k


def hg_oml(c, lbrep_d):
    b = c.b
    c.oml = b.sb([128, 2, 1024], F32, "oml")
    c.oml_trk = Trk()
    with c.arena as es:
        e = b.sb([128, 8, 1024], F32, "lbe", es)
        e_trk = Trk()
        for j in range(8):
            b.dma(SP, e[:, j, :], lbrep_d[:, j, :], writes=[e_trk])
        for j in range(8):
            b.act(e[:, j, :], e[:, j, :], AF.Exp, [e_trk], [e_trk])
        for dr in range(2):
            den = c.oml[:, dr, :]
            b.tt(DVE, den, e[:, dr * 4, :], e[:, dr * 4 + 1, :], ALU.add, [e_trk], [c.oml_trk])
            b.tt(DVE, den, den, e[:, dr * 4 + 2, :], ALU.add, [e_trk, c.oml_trk], [c.oml_trk])
            b.tt(DVE, den, den, e[:, dr * 4 + 3, :], ALU.add, [e_trk, c.oml_trk], [c.oml_trk])
            b.recip(den, den, [c.oml_trk], [c.oml_trk])
            b.tt(DVE, den, den, e[:, dr * 4, :], ALU.mult, [e_trk, c.oml_trk], [c.oml_trk])
        b.barrier()


def hg_project(c, ntile, wqv_d, wf_d, scr_qv, scr_f, scr_trk, wog_d=None, scr_og=None):
    b = c.b
    TT = ntile * 128
    with c.arena as es:
        hT = b.sb([128, 8, TT], BF16, "hT", es)
        h_trk = Trk()
        wts = b.sb([128, 8 * 2048], BF16, "wts", es)
        w_trk = Trk()
        sgb = [(b.sb([128, 512], BF16, "sgb", es), Trk()) for _ in range(2)]
        sgf = [(b.sb([128, 512], F32, "sgf", es), Trk()) for _ in range(2)]
        st = {"i": 0}
        tl = tiles_of(TT)
        norm_mod(c, hT, h_trk, 0, [(t0, n, (1 if t0 >= NT else 0), t0) for (t0, n) in tl])
        if wog_d is not None:
            def evac_og(f, ti, t0, n, ps, ps_trk):
                sg, sg_trk = sgb[st["i"] % 2]
                st["i"] += 1
                b.act(sg[:, 0:n], ps[:, 0:n], AF.Silu, [ps_trk], [sg_trk])
                b.dma(SP, scr_og[f, :, t0:t0 + n], sg[:, 0:n], reads=[sg_trk], writes=[scr_trk], st=sg_trk)
            linear_fm(c, wog_d, 8, 1, 8, hT, h_trk, tl, evac_og)
        for half, (wd, scr, isf) in enumerate([(wqv_d, scr_qv, False), (wf_d, scr_f, True)]):
            for k in range(8):
                b.dma(POOL, wts[:, k * 2048:(k + 1) * 2048], wd[:, k * 2048:(k + 1) * 2048], writes=[w_trk])
            for kt in range(ntile):
                tk = kt * 128
                for fc in range(4):
                    ps, ps_trk = b.nextbank()
                    for k in range(8):
                        b.mm(ps[:, 0:512], hT[:, k, tk:tk + 128], wts[:, k * 2048 + fc * 512:k * 2048 + fc * 512 + 512],
                             k == 0, k == 7, [h_trk, w_trk], [ps_trk], inc=(k == 7))
                    sg, sg_trk = (sgf if isf else sgb)[st["i"] % 2]
                    st["i"] += 1
                    if st["i"] % 2:
                        b.copy(DVE, sg[:], ps[:, 0:512], [ps_trk], [sg_trk])
                    else:
                        b.op(ACT, lambda e, o=sg[:], i=ps[:, 0:512]: e.copy(o, i), reads=[ps_trk], writes=[sg_trk])
                    b.dma(SP, scr[kt, :, fc * 512:(fc + 1) * 512], sg[:], reads=[sg_trk], writes=[scr_trk], st=sg_trk)
        b.barrier()


def hg_alloc_wide(c, es, with_q):
    b = c.b
    w = Ctx()
    w.fp = [(b.sb([128, 1024], F32, "fp", es), Trk()) for _ in range(2)]
    w.qv = [(b.sb([128, 2048], BF16, "qv", es), Trk()) for _ in range(2)]
    w.L = b.sb([128, 1024], F32, "L", es)
    w.Enb = b.sb([128, 1024], F32, "Enb", es)
    w.Etot = b.sb([128, 1024], F32, "Etot", es)
    w.kh = b.sb([128, 1024], BF16, "kh", es)
    w.wt = Trk()
    if with_q:
        w.Eb = b.sb([128, 1024], F32, "Eb", es)
        w.qs = b.sb([128, 1024], BF16, "qs", es)
        w.qt = b.sb([128, 1024], BF16, "qt", es)
        w.kt = b.sb([128, 1024], BF16, "kt", es)
    w.i = 0
    return w


def hg_wide(c, w, k, dr, kt, scr_qv, scr_f, scr_trk, with_q):
    b = c.b
    fp, fp_trk = w.fp[w.i % 2]
    qv, qv_trk = w.qv[w.i % 2]
    w.i += 1
    b.dma(SP, fp[:], scr_f[kt, :, dr * 1024:(dr + 1) * 1024], reads=[scr_trk], writes=[fp_trk])
    b.dma(SP, qv[:], scr_qv[kt], reads=[scr_trk], writes=[qv_trk])
    b.act(fp[:], fp[:], AF.Sigmoid, [fp_trk], [fp_trk], scale=-1.0)
    b.tt(DVE, fp[:], fp[:], c.oml[:, dr, :], ALU.mult, [fp_trk, c.oml_trk], [fp_trk])
    b.act(w.L[:], fp[:], AF.Ln, [fp_trk], [w.wt], bias=1.0, scale=-1.0)
    bps = b.pst[0]
    tps = b.pst[1]
    b_trks = [b.bank_trk[0], b.bank_trk[1]]
    t_trks = [b.bank_trk[2], b.bank_trk[3]]
    for hf in range(2):
        b.mm(bps[:, hf * 512:(hf + 1) * 512], k.trif[:, dr, :], w.L[:, hf * 512:(hf + 1) * 512], True, True,
             [k.trk, w.wt], b_trks, inc=False)
    for hf in range(2):
        b.mm(tps[:, hf * 512:(hf + 1) * 512], k.onesf[:], w.L[:, hf * 512:(hf + 1) * 512], True, True,
             [k.trk, w.wt], t_trks, inc=(hf == 1))
    b.act(w.Enb[:], bps[:], AF.Exp, b_trks, [w.wt], scale=-1.0)
    b.act(w.Etot[:], tps[:], AF.Exp, t_trks, [w.wt])
    if with_q:
        b.act(w.Eb[:], bps[:], AF.Exp, b_trks, [w.wt])
        b.act(w.qs[:], qv[:, 0:1024], AF.Silu, [qv_trk], [w.wt])
        b.tt(DVE, w.qt[:], w.qs[:], w.Eb[:], ALU.mult, [w.wt], [w.wt])
    b.tt(DVE, w.Enb[:], w.Enb[:], fp[:], ALU.mult, [w.wt, fp_trk], [w.wt])
    if with_q:
        b.copy(DVE, w.kt[:], w.Enb[:], [w.wt], [w.wt])
    b.tt(DVE, w.kh[:], w.Enb[:], w.Etot[:], ALU.mult, [w.wt], [w.wt])
    return qv, qv_trk


def hg_state_update(c, w, k, S, s_trk, decT, dec_trk, hd, h, v_ap, v_trk, Lsum=None, l_trk=None):
    b = c.b
    dps, dps_trk = b.bank(4)
    b.mm(dps[:, hd:hd + 1], w.L[:, h * 128:(h + 1) * 128], k.onesf[:, 0:1], True, True, [w.wt, k.trk], [dps_trk],
         inc=True)
    b.act(decT[:, hd:hd + 1], dps[:, hd:hd + 1], AF.Exp, [dps_trk], [dec_trk])
    if Lsum is not None:
        b.tt(DVE, Lsum[:, hd:hd + 1], Lsum[:, hd:hd + 1], dps[:, hd:hd + 1], ALU.add, [dps_trk, l_trk], [l_trk])
    sps, sps_trk = b.bank(5 + hd % 2)
    b.mm(sps[:, 0:128], w.kh[:, h * 128:(h + 1) * 128], v_ap, True, True, [w.wt, v_trk], [sps_trk], inc=True)
    b.stt(DVE, S[:, hd, :], S[:, hd, :], decT[:, hd:hd + 1], sps[:, 0:128], ALU.mult, ALU.add,
          [s_trk, dec_trk, sps_trk], [s_trk])


def hg_din(b, c):
    c.tri_d = b.din("tri", [128, 2, 128])
    c.ident_d = b.din("ident", [128, 128])


def build_A3():
    b = Builder()
    TT = NT + NCX
    xT_d = b.din("xT", [128, 8, TT])
    d = common_inputs(b)
    wqv_d = b.din("wqv", [128, 8 * 2048])
    wf_d = b.din("wf", [128, 8 * 2048])
    lbrep_d = b.din("lbrep", [128, 8, 1024])
    sloc_o, sloc_trk = b.dout("S_loc", [128, 16, 128])
    sctx_o, sctx_trk = b.dout("S_ctx", [128, 16, 128])
    dloc_o, dloc_trk = b.dout("D_loc", [128, 16])
    mod_o, mod_trk = b.dout("modT_o", [128, 96])
    scr_qv = b.dscr("scr_qv", [18, 128, 2048], BF16)
    scr_f = b.dscr("scr_f", [18, 128, 2048], F32)
    scr_trk = Trk()
    c = setup_common(b, TT, 20500)
    hg_din(b, c)
    for ch in range(8):
        b.dma(SP, c.xT[:, ch, :], xT_d[:, ch, :], writes=[c.x_trk])
    hg_oml(c, lbrep_d)
    with c.arena as es:
        compute_mods(c, d["ada_w"], d["ada_bT"], d["ccT"], es)
        make_AB(c, d["ng1"], 0, es)
        b.dma(SP, mod_o, c.modT[:].rearrange("p j w -> p (j w)"), reads=[c.mod_trk], writes=[mod_trk], st=c.mod_trk)
        b.barrier()
    hg_project(c, 18, wqv_d, wf_d, scr_qv, scr_f, scr_trk)
    with c.arena as es:
        k = hg_consts(c, es, b)
        w = hg_alloc_wide(c, es, False)
        S = b.sb([128, 16, 128], F32, "S", es)
        s_trk = Trk()
        decT = b.sb([128, 16], F32, "decT", es)
        dec_trk = Trk()
        Lsum = b.sb([128, 16], F32, "Lsum", es)
        l_trk = Trk()
        b.memset(DVE, S[:], 0.0, [s_trk])
        for phase in range(2):
            for dr in range(2):
                if phase == 0:
                    order = [16, 17] if dr == 0 else [17, 16]
                else:
                    order = list(range(16)) if dr == 0 else list(range(15, -1, -1))
                for kt in order:
                    qv, qv_trk = hg_wide(c, w, k, dr, kt, scr_qv, scr_f, scr_trk, False)
                    for h in range(8):
                        hg_state_update(c, w, k, S, s_trk, decT, dec_trk, dr * 8 + h, h,
                                        qv[:, 1024 + h * 128:1024 + (h + 1) * 128], qv_trk,
                                        Lsum if phase == 1 else None, l_trk)
            if phase == 0:
                b.dma(SP, sctx_o, S[:], reads=[s_trk], writes=[sctx_trk], st=s_trk)
                b.memset(DVE, S[:], 0.0, [s_trk])
                b.memset(DVE, Lsum[:], 0.0, [l_trk])
        b.act(Lsum[:], Lsum[:], AF.Exp, [l_trk], [l_trk])
        b.dma(SP, sloc_o, S[:], reads=[s_trk], writes=[sloc_trk], st=s_trk)
        b.dma(SP, dloc_o, Lsum[:], reads=[l_trk], writes=[dloc_trk], st=l_trk)
        b.barrier()
    return b.finish()


def build_B3():
    b = Builder()
    TT = NT
    xT_d = b.din("xT", [128, 8, TT])
    d = common_inputs(b)
    mod_d = b.din("modT", [128, 96])
    wqv_d = b.din("wqv", [128, 8 * 2048])
    wf_d = b.din("wf", [128, 8 * 2048])
    wog_d = b.din("wog", [8, 128, 1024])
    lbrep_d = b.din("lbrep", [128, 8, 1024])
    sctx_d = b.din("S_ctx", [128, 16, 128])
    dp_d = b.din("Dp", [128, 7, 16])
    sp_d = b.din("Sp", [7, 128, 16, 128])
    gn_d = b.din("gnorm", [128, 1])
    wo_d = b.din("wo", [8, 128, 1024])
    fg_d = b.din("final_g", [128, 8])
    out_d, out_trk = b.dout("outT", [128, 8, NT])
    scr_qv = b.dscr("scr_qv", [16, 128, 2048], BF16)
    scr_f = b.dscr("scr_f", [16, 128, 2048], F32)
    scr_og = b.dscr("scr_og", [8, 128, NT], BF16)
    scr_trk = Trk()
    c = setup_common(b, TT, 25600)
    hg_din(b, c)
    for ch in range(8):
        b.dma(SP, c.xT[:, ch, :], xT_d[:, ch, :], writes=[c.x_trk])
    b.dma(SP, c.modT[:].rearrange("p j w -> p (j w)"), mod_d, writes=[c.mod_trk])
    hg_oml(c, lbrep_d)
    with c.arena as es:
        make_AB(c, d["ng1"], 0, es)
        b.barrier()
    hg_project(c, 16, wqv_d, wf_d, scr_qv, scr_f, scr_trk, wog_d, scr_og)
    with c.arena as es:
        k = hg_consts(c, es, b)
        w = hg_alloc_wide(c, es, True)
        S = b.sb([128, 16, 128], F32, "S", es)
        s_trk = Trk()
        Sb = b.sb([128, 16, 128], BF16, "Sb", es)
        sb_trk = Trk()
        decT = b.sb([128, 16], F32, "decT", es)
        dec_trk = Trk()
        oT = b.sb([128, 8, NT], BF16, "oT", es)
        o_trk = Trk()
        dp = b.sb([128, 7, 16], F32, "dp", es)
        dp_trk = Trk()
        gn = b.sb([128, 1], F32, "gn", es)
        gn_trk = Trk()
        ogr = [(b.sb([128, 128], BF16, "og", es), Trk()) for _ in range(2)]
        tr = [(b.sb([128, 2, 128], BF16, "tr", es), Trk()) for _ in range(2)]
        am = [(b.sb([128, 128], BF16, "am", es), Trk()) for _ in range(2)]
        of = [(b.sb([128, 128], F32, "of", es), Trk()) for _ in range(2)]
        sq = [(b.sb([128, 128], BF16, "sqo", es), Trk()) for _ in range(2)]
        rs = [(b.sb([128, 128], F32, "rs", es), Trk()) for _ in range(2)]
        b.dma(SP, dp[:], dp_d, writes=[dp_trk])
        b.dma(SP, gn[:], gn_d, writes=[gn_trk])
        b.dma(SP, S[:], sctx_d, writes=[s_trk])
        for step in range(7):
            for half in range(2):
                stg, stg_trk = w.fp[(step * 2 + half) % 2]
                b.dma(SP, stg[:], sp_d[step, :, half * 8:(half + 1) * 8, :].rearrange("p h e -> p (h e)"), writes=[stg_trk])
                for hh in range(8):
                    hd = half * 8 + hh
                    b.stt(DVE, S[:, hd, :], S[:, hd, :], dp[:, step, hd:hd + 1], stg[:, hh * 128:(hh + 1) * 128],
                          ALU.mult, ALU.add, [s_trk, dp_trk, stg_trk], [s_trk])
        b.copy(DVE, Sb[:], S[:], [s_trk], [sb_trk])
        cnt = {"u": 0}
        for dr in range(2):
            order = list(range(16)) if dr == 0 else list(range(15, -1, -1))
            for kt in order:
                qv, qv_trk = hg_wide(c, w, k, dr, kt, scr_qv, scr_f, scr_trk, True)
                for h in range(8):
                    hd = dr * 8 + h
                    u = cnt["u"]
                    cnt["u"] += 1
                    hs = slice(h * 128, (h + 1) * 128)
                    tp, tp_trk = b.bank(6)
                    b.mm(tp[:, 0:128], w.kt[:, hs], k.ident[:], True, True, [w.wt, k.trk], [tp_trk], inc=False)
                    b.mm(tp[:, 128:256], w.qt[:, hs], k.ident[:], True, True, [w.wt, k.trk], [tp_trk], inc=True)
                    t_, t_trk = tr[u % 2]
                    b.op(ACT, lambda e, o=t_[:].rearrange("p a s -> p (a s)"), i=tp[:, 0:256]: e.copy(o, i),
                         reads=[tp_trk], writes=[t_trk])
                    ap_, ap_trk = b.bank(7)
                    b.mm(ap_[:, 0:128], t_[:, 0, :], t_[:, 1, :], True, True, [t_trk], [ap_trk], inc=True)
                    a_, a_trk = am[u % 2]
                    b.tt(DVE, a_[:], ap_[:, 0:128], k.maskb[:, dr, :], ALU.mult, [ap_trk, k.trk], [a_trk])
                    op_, op_trk = b.bank(5 + 0)
                    op_, op_trk = b.pst[3][:, 512 + 256:512 + 384], b.bank_trk[7]
                    b.mm(op_, qv[:, 1024 + h * 128:1024 + (h + 1) * 128], a_[:], True, False, [qv_trk, a_trk], [op_trk],
                         inc=False)
                    b.mm(op_, Sb[:, hd, :], t_[:, 1, :], False, True, [sb_trk, t_trk], [op_trk], inc=True)
                    if dr == 0:
                        b.op(ACT, lambda e, o=oT[:, h, kt * 128:(kt + 1) * 128], i=op_: e.copy(o, i),
                             reads=[op_trk], writes=[o_trk])
                    else:
                        o_, of_trk = of[u % 2]
                        b.tt(DVE, o_[:], op_, oT[:, h, kt * 128:(kt + 1) * 128], ALU.add, [op_trk, o_trk], [of_trk])
                        s_, sq_trk = sq[u % 2]
                        b.act(s_[:], o_[:], AF.Square, [of_trk], [sq_trk])
                        np_, np_trk = b.pst[3][:, 512 + 384:512 + 512], b.bank_trk[7]
                        b.mm(np_, c.ones[:], s_[:], True, True, [c.ones_trk, sq_trk], [np_trk], inc=True)
                        r_, r_trk = rs[u % 2]
                        b.act(r_[:], np_, AF.Sqrt, [np_trk, c.eps_trk], [r_trk], bias=c.eps[:, 0:1], scale=1.0 / 128)
                        b.recip(r_[:], r_[:], [r_trk], [r_trk])
                        g_, g_trk = ogr[u % 2]
                        b.dma(SP, g_[:], scr_og[h, :, kt * 128:(kt + 1) * 128], reads=[scr_trk], writes=[g_trk])
                        b.tt(DVE, o_[:], o_[:], r_[:], ALU.mult, [of_trk, r_trk], [of_trk])
                        b.stt(DVE, oT[:, h, kt * 128:(kt + 1) * 128], o_[:], gn[:, 0:1], g_[:], ALU.mult, ALU.mult,
                              [of_trk, gn_trk, g_trk], [o_trk])
                    hg_state_update(c, w, k, S, s_trk, decT, dec_trk, hd, h,
                                    qv[:, 1024 + h * 128:1024 + (h + 1) * 128], qv_trk)
                    b.op(ACT, lambda e, o=Sb[:, hd, :], i=S[:, hd, :]: e.copy(o, i), reads=[s_trk], writes=[sb_trk])

        def evac_out(f, ti, t0, n, ps, ps_trk):
            b.stt(DVE, c.xT[:, f, t0:t0 + n], ps[:, 0:n], c.modT[:, 16 + f, 0:1], c.xT[:, f, t0:t0 + n],
                  ALU.mult, ALU.add, [ps_trk, c.mod_trk, c.x_trk], [c.x_trk])

        linear_fm(c, wo_d, 8, 1, 8, oT, o_trk, tiles_of(NT), evac_out, banks=(0, 4))
        b.barrier()
    ffn(c, d["ng2"], d["ffn_w_in"], d["ffn_w_out"], supers=[[(0, 512, 0), (512, 512, 0)], [(1024, 512, 0), (1536, 512, 0)]])
    with c.arena as es:
        fg = b.sb([128, 8], F32, "fg", es)
        c.g_trk = Trk()
        b.dma(SP, fg[:], fg_d, writes=[c.g_trk])
        ot = b.sb([128, 8, NT], F32, "ot", es)
        ot_trk = Trk()
        norm_mod(c, ot, ot_trk, 0, [(t0, n, 0, t0) for (t0, n) in tiles_of(NT)], plain_g=fg)
        for ch in range(8):
            b.dma(SP, out_d[:, ch, :], ot[:, ch, :], reads=[ot_trk], writes=[out_trk], st=ot_trk)
    return b.finish()


def hg_host_common(inp):
    W = inp["hg_w_in"][0]
    q, ffw, fbw, v, og = [W[:, i * 1024:(i + 1) * 1024] for i in range(5)]

    def tmaj(M):
        return np.ascontiguousarray(M.reshape(8, 128, M.shape[1]).transpose(1, 0, 2).reshape(128, 8 * M.shape[1]))
    s = np.arange(128)[:, None]
    t = np.arange(128)[None, :]
    tri = np.stack([(s <= t), (s >= t)], axis=1).astype(np.float32)
    lb = inp["hg_lb"].reshape(8, 1024)
    return {
        "wqv": tmaj(np.concatenate([q, v], axis=1)),
        "wf": tmaj(np.concatenate([ffw, fbw], axis=1)),
        "wog": lhsT_blocks(og, 1),
        "lbrep": np.ascontiguousarray(np.broadcast_to(lb[None], (128, 8, 1024))),
        "tri": np.ascontiguousarray(tri),
        "ident": np.eye(128, dtype=np.float32),
    }


def run_layer3(xs, ctx, inp):
    common = common_host(inp, 3)
    hc = hg_host_common(inp)
    a_common = dict(common)
    a_common.update({k: hc[k] for k in ("wqv", "wf", "lbrep", "tri", "ident")})
    maps = []
    for cid in range(NCORES):
        lo, hi = cid * NT, (cid + 1) * NT
        m = dict(a_common)
        m["xT"] = toT(np.concatenate([xs[lo:hi], ctx], axis=0))
        maps.append(m)
    resA = run_bass_kernel_spmd(build_A3(), maps, core_ids=list(range(NCORES))).results
    b_common = dict(common)
    b_common.update(hc)
    b_common.update({
        "S_ctx": resA[0]["S_ctx"],
        "gnorm": np.ascontiguousarray(inp["hg_gnorm_g"][0].reshape(128, 1)),
        "wo": lhsT_blocks(inp["hg_wo"][0], 1),
        "final_g": fm(inp["final_g"]),
    })
    one = np.ones((128, 8), np.float32)
    zero = np.zeros((128, 8, 128), np.float32)
    mapsB = []
    for cid in range(NCORES):
        fw = [None] * (7 - cid) + list(range(0, cid))
        bw = [None] * cid + list(range(NCORES - 1, cid, -1))
        Dp = np.zeros((128, 7, 16), np.float32)
        Sp = np.zeros((7, 128, 16, 128), np.float32)
        for st in range(7):
            Dp[:, st, 0:8] = one if fw[st] is None else resA[fw[st]]["D_loc"][:, 0:8]
            Dp[:, st, 8:16] = one if bw[st] is None else resA[bw[st]]["D_loc"][:, 8:16]
            Sp[st, :, 0:8] = zero if fw[st] is None else resA[fw[st]]["S_loc"][:, 0:8]
            Sp[st, :, 8:16] = zero if bw[st] is None else resA[bw[st]]["S_loc"][:, 8:16]
        m = dict(b_common)
        m["xT"] = toT(xs[cid * NT:(cid + 1) * NT])
        m["modT"] = resA[cid]["modT_o"]
        m["Dp"], m["Sp"] = Dp, Sp
        mapsB.append(m)
    resB = run_bass_kernel_spmd(build_B3(), mapsB, core_ids=list(range(NCORES))).results
    return np.concatenate([fromT(r["outT"]) for r in resB], axis=0)


def kernel(**inp):
    inp = {k: np.asarray(v) for k, v in inp.items()}
    xs = np.ascontiguousarray(inp["x"][0], dtype=np.float32)
    ctx = np.ascontiguousarray(inp["ctx"][0], dtype=np.float32)
    xs, ctx = run_layer0(xs, ctx, inp)
    xs, ctx = run_layer1(xs, ctx, inp)
    xs, ctx = run_layer2(xs, ctx, inp)
    out = run_layer3(xs, ctx, inp)
    return np.ascontiguousarray(out[None].astype(np.float32))
```

```python
import math
from contextlib import ExitStack
import numpy as np
import ml_dtypes
import concourse.bass as bass
import concourse.mybir as mybir
from concourse.bass_utils import run_bass_kernel_spmd

F32 = mybir.dt.float32
BF16 = mybir.dt.bfloat16
AF = mybir.ActivationFunctionType
ALU = mybir.AluOpType
NPBF = ml_dtypes.bfloat16

NCORES = 8
D = 1024
SEQ = 16384
NT = SEQ // NCORES
NCX = 256
DFF = 2816
EPS = 1e-6
PE, ACT, DVE, POOL, SP = "pe", "act", "dve", "pool", "sp"
ENGS = (PE, ACT, DVE, POOL, SP)


class Sem:
    def __init__(self, h):
        self.h = h
        self.n = 0


class Trk:
    __slots__ = ("w", "r", "dsem")

    def __init__(self):
        self.w = {}
        self.r = {}
        self.dsem = None


class Builder:
    def __init__(self):
        self.nc = bass.Bass("TRN2", target_bir_lowering=False)
        self.es = ExitStack()
        self.prog = {e: [] for e in ENGS}
        self.seen = {e: {} for e in ENGS}
        self.nsem = 0
        self.esem = {e: self.newsem(e) for e in (PE, ACT, DVE, POOL)}
        self.dsems = []
        self.same_sync = True
        self.uid = 0
        self.outs = []
        self.pst = [self.es.enter_context(self.nc.psum_tensor(f"ps{i}", [128, 1024], F32)) for i in range(4)]
        self.bank_trk = [Trk() for _ in range(8)]
        self.bank_i = 0

    def newsem(self, name):
        self.nsem += 1
        return Sem(self.es.enter_context(self.nc.semaphore(f"s{self.nsem}_{name}")))

    def sb(self, shape, dt, name=None, es=None):
        self.uid += 1
        if es is not None and hasattr(es, "get"):
            return es.get(shape, dt)
        return (es or self.es).enter_context(self.nc.sbuf_tensor(f"{name or 't'}{self.uid}", list(shape), dt))

    def din(self, name, shape, dt=F32):
        return self.nc.dram_tensor(name, list(shape), dt, kind="ExternalInput").ap()

    def dout(self, name, shape, dt=F32):
        ap = self.nc.dram_tensor(name, list(shape), dt, kind="ExternalOutput").ap()
        t = Trk()
        self.outs.append(t)
        return ap, t

    def dscr(self, name, shape, dt=BF16):
        return self.nc.dram_tensor(name, list(shape), dt, kind="Internal").ap()

    def bank(self, i):
        return self.pst[i // 2][:, (i % 2) * 512:(i % 2) * 512 + 512], self.bank_trk[i]

    def nextbank(self, lo=0, hi=8):
        i = lo + (self.bank_i % (hi - lo))
        self.bank_i += 1
        return self.bank(i)

    def _wait(self, eng, sp):
        sem, val = sp
        if sem is self.esem.get(eng) and (eng == PE or not self.same_sync):
            return
        if self.seen[eng].get(sem, 0) >= val:
            return
        self.seen[eng][sem] = val
        self.prog[eng].append(lambda e, s=sem.h, v=val: e.wait_ge(s, v))

    def _deps(self, eng, reads, writes):
        for t in reads:
            for sp in t.w.values():
                self._wait(eng, sp)
        for t in writes:
            for sp in t.w.values():
                self._wait(eng, sp)
            for sp in t.r.values():
                self._wait(eng, sp)

    def op(self, eng, fn, reads=(), writes=(), inc=True):
        self._deps(eng, reads, writes)
        sem = self.esem[eng]
        if inc:
            sem.n += 1
            v = sem.n
            self.prog[eng].append(lambda e, f=fn, s=sem.h: f(e).then_inc(s, 1))
        else:
            v = sem.n + 1
            self.prog[eng].append(lambda e, f=fn: f(e))
        sp = (sem, v)
        for t in reads:
            t.r[eng] = sp
        for t in writes:
            t.w[sem] = sp
            t.r = {}

    def dma(self, q, out, in_, reads=(), writes=(), st=None):
        self._deps(q, reads, writes)
        st = st or (writes[0] if writes else reads[0])
        if st.dsem is None:
            st.dsem = self.newsem("d")
            self.dsems.append(st.dsem)
        ds = st.dsem
        ds.n += 16
        sp = (ds, ds.n)
        self.prog[q].append(lambda e, o=out, i=in_, s=ds.h: e.dma_start(out=o, in_=i).then_inc(s, 16))
        for t in reads:
            t.r[ds] = sp
        for t in writes:
            t.w[ds] = sp
            t.r = {}

    def barrier(self):
        for e in ENGS:
            for e2 in (PE, ACT, DVE, POOL):
                if e2 != e and self.esem[e2].n > 0:
                    self._wait(e, (self.esem[e2], self.esem[e2].n))
            for ds in self.dsems:
                if ds.n > 0:
                    self._wait(e, (ds, ds.n))

    def mm(self, out, lhsT, rhs, start, stop, reads, writes, inc):
        self.op(PE, lambda e, o=out, l=lhsT, r=rhs, a=start, b=stop: e.matmul(o, l, r, start=a, stop=b),
                reads=reads, writes=writes, inc=inc)

    def act(self, out, in_, func, reads, writes, bias=0.0, scale=1.0, accum_out=None):
        def f(e, o=out, i=in_, fu=func, b=bias, s=scale, a=accum_out):
            if a is None:
                return e.activation(out=o, in_=i, func=fu, bias=b, scale=s)
            return e.activation(out=o, in_=i, func=fu, bias=b, scale=s, accum_out=a)
        self.op(ACT, f, reads=reads, writes=writes)

    def tt(self, eng, out, in0, in1, op, reads, writes):
        self.op(eng, lambda e, o=out, a=in0, b=in1, p=op: e.tensor_tensor(o, a, b, p), reads=reads, writes=writes)

    def ts(self, eng, out, in0, s1, s2, op0, op1, reads, writes):
        if s2 is None:
            self.op(eng, lambda e, o=out, a=in0, x=s1, p=op0: e.tensor_scalar(o, a, x, None, p),
                    reads=reads, writes=writes)
        else:
            self.op(eng, lambda e, o=out, a=in0, x=s1, y=s2, p=op0, q=op1: e.tensor_scalar(o, a, x, y, p, q),
                    reads=reads, writes=writes)

    def stt(self, eng, out, in0, scalar, in1, op0, op1, reads, writes):
        self.op(eng, lambda e, o=out, a=in0, s=scalar, b=in1, p=op0, q=op1: e.scalar_tensor_tensor(o, a, s, b, p, q),
                reads=reads, writes=writes)

    def copy(self, eng, out, in_, reads, writes):
        self.op(eng, lambda e, o=out, i=in_: e.tensor_copy(o, i), reads=reads, writes=writes)

    def recip(self, out, in_, reads, writes):
        self.op(DVE, lambda e, o=out, i=in_: e.reciprocal(o, i), reads=reads, writes=writes)

    def memset(self, eng, ap, val, writes):
        self.op(eng, lambda e, a=ap, v=val: e.memset(a, v), writes=writes)

    def finish(self):
        for t in self.outs:
            for sp in t.w.values():
                self._wait(SP, sp)
        nc = self.nc
        prog = self.prog
        with nc.Block() as block:
            @block.sync
            def _(e):
                for f in prog[SP]:
                    f(e)

            @block.tensor
            def _(e):
                for f in prog[PE]:
                    f(e)

            @block.scalar
            def _(e):
                for f in prog[ACT]:
                    f(e)

            @block.vector
            def _(e):
                for f in prog[DVE]:
                    f(e)

            @block.gpsimd
            def _(e):
                for f in prog[POOL]:
                    f(e)
        self.es.close()
        return nc


def lhsT_blocks(W, G=1):
    K, F = W.shape
    KC, NF = K // 128, F // 128
    a = W.reshape(KC, 128, NF, 128).transpose(2, 1, 0, 3).reshape(NF // G, G, 128, KC * 128)
    a = a.transpose(0, 2, 1, 3).reshape(NF // G, 128, G * KC * 128)
    return np.ascontiguousarray(a)


def fm(v):
    v = np.asarray(v)
    lead = v.shape[:-1]
    a = v.reshape(lead + (v.shape[-1] // 128, 128))
    a = np.moveaxis(a, -1, 0)
    return np.ascontiguousarray(a)


def toT(x):
    T, Dm = x.shape
    return np.ascontiguousarray(x.reshape(T, Dm // 128, 128).transpose(2, 1, 0))


def fromT(xT):
    p, c, T = xT.shape
    return np.ascontiguousarray(xT.transpose(2, 1, 0).reshape(T, c * p))


class Arena:
    def __init__(self, b, nf32):
        self.t = b.sb([128, nf32], F32, "arena")
        self.n = nf32
        self.off = 0

    def __enter__(self):
        self.off = 0
        return self

    def __exit__(self, *a):
        return False

    def get(self, shape, dt):
        nel = 1
        for s_ in shape[1:]:
            nel *= s_
        nf = (nel + 1) // 2 if dt == BF16 else nel
        nf = (nf + 7) // 8 * 8
        assert self.off + nf <= self.n, f"arena overflow {self.off}+{nf}>{self.n}"
        ap = self.t[:, self.off:self.off + nf]
        self.off += nf
        if dt == BF16:
            ap = ap.bitcast(BF16)[:, 0:nel]
        else:
            ap = ap[:, 0:nel]
        if len(shape) == 3:
            ap = ap.rearrange("p (a b) -> p a b", a=shape[1])
        return ap


class Ctx:
    pass


def tiles_of(total, start=0, step=512):
    out = []
    t = start
    while t < start + total:
        n = min(step, start + total - t)
        out.append((t, n))
        t += n
    return out


def setup_common(b, TT, arena_f32=25088):
    c = Ctx()
    c.b = b
    c.TT = TT
    c.xT = b.sb([128, 8, TT], F32, "xT")
    c.x_trk = Trk()
    c.ones = b.sb([128, 128], BF16, "ones")
    c.ones_trk = Trk()
    b.memset(DVE, c.ones[:], 1.0, [c.ones_trk])
    c.wslots = [(b.sb([128, 3072], BF16, "w"), Trk()) for _ in range(3)]
    c.wi = 0
    c.arena = Arena(b, arena_f32)
    c.modT = b.sb([128, 48, 2], F32, "modT")
    c.mod_trk = Trk()
    c.A = b.sb([128, 8, 2], F32, "A")
    c.A_trk = Trk()
    c.sq = b.sb([128, 8, 512], BF16, "sq")
    c.sq_trk = Trk()
    c.rstd = b.sb([128, 512], F32, "rstd")
    c.rstd_trk = Trk()
    c.tmp = [(b.sb([128, 512], F32, "tmp"), Trk()) for _ in range(2)]
    c.tmpi = 0
    c.eps = b.sb([128, 1], F32, "eps")
    c.eps_trk = Trk()
    b.memset(DVE, c.eps[:], EPS, [c.eps_trk])
    return c


def wslot(c):
    s = c.wslots[c.wi % len(c.wslots)]
    c.wi += 1
    return s


def load_mods(c, mod_d):
    b = c.b
    b.dma(SP, c.modT[:], mod_d, writes=[c.mod_trk])


def compute_mods(c, ada_w_d, ada_bT_d, ccT_d, es):
    b = c.b
    cc = b.sb([128, 8, 2], F32, "cc", es)
    cc_trk = Trk()
    ab = b.sb([128, 48, 2], F32, "ab", es)
    ab_trk = Trk()
    b.dma(SP, cc[:], ccT_d, writes=[cc_trk])
    b.dma(SP, ab[:], ada_bT_d, writes=[ab_trk])
    b.act(cc[:], cc[:], AF.Silu, [cc_trk], [cc_trk])
    aw = [(b.sb([128, 8, 256], F32, "aw", es), Trk()) for _ in range(2)]
    ps, ps_trk = b.bank(7)
    for g in range(24):
        t, tr = aw[g % 2]
        b.dma(SP, t[:], ada_w_d[:, g * 256:(g + 1) * 256].rearrange("(kc p) f -> p kc f", p=128), writes=[tr])
        for gi in range(2):
            f = g * 2 + gi
            for k in range(8):
                b.mm(ps[:, 2 * f:2 * f + 2], t[:, k, gi * 128:(gi + 1) * 128], cc[:, k, :], k == 0, k == 7,
                     [tr, cc_trk], [ps_trk], inc=(k == 7))
    b.tt(DVE, c.modT[:].rearrange("p j w -> p (j w)"), ps[:, 0:96], ab[:].rearrange("p j w -> p (j w)"), ALU.add,
         [ps_trk, ab_trk], [c.mod_trk])


def make_AB(c, ngT_d, which, es):
    b = c.b
    ng = b.sb([128, 8], F32, "ng", es)
    ng_trk = Trk()
    b.dma(SP, ng[:], ngT_d, writes=[ng_trk])
    base = which * 24 + 8
    for w in range(2):
        b.stt(DVE, c.A[:, :, w], c.modT[:, base:base + 8, w], 1.0, ng[:], ALU.add, ALU.mult,
              [c.mod_trk, ng_trk], [c.A_trk])


def norm_mod(c, outT, otrk, which, tiles, plain_g=None, cb=None):
    b = c.b
    for (t0, n, w, o) in tiles:
        for ch in range(8):
            b.act(c.sq[:, ch, 0:n], c.xT[:, ch, t0:t0 + n], AF.Square, [c.x_trk], [c.sq_trk])
        ps, ps_trk = b.nextbank()
        for ch in range(8):
            b.mm(ps[:, 0:n], c.ones[:], c.sq[:, ch, 0:n], ch == 0, ch == 7, [c.ones_trk, c.sq_trk], [ps_trk],
                 inc=(ch == 7))
        b.act(c.rstd[:, 0:n], ps[:, 0:n], AF.Sqrt, [ps_trk, c.eps_trk], [c.rstd_trk], bias=c.eps[:, 0:1],
              scale=1.0 / D)
        b.recip(c.rstd[:, 0:n], c.rstd[:, 0:n], [c.rstd_trk], [c.rstd_trk])
        for ch in range(8):
            tmp, tmp_trk = c.tmp[c.tmpi % 2]
            c.tmpi += 1
            b.tt(DVE, tmp[:, 0:n], c.xT[:, ch, t0:t0 + n], c.rstd[:, 0:n], ALU.mult,
                 [c.x_trk, c.rstd_trk], [tmp_trk])
            if plain_g is None:
                b.act(outT[:, ch, o:o + n], tmp[:, 0:n], AF.Identity, [tmp_trk, c.A_trk, c.mod_trk], [otrk],
                      bias=c.modT[:, which * 24 + ch, w:w + 1], scale=c.A[:, ch, w:w + 1])
            else:
                b.act(outT[:, ch, o:o + n], tmp[:, 0:n], AF.Copy, [tmp_trk, c.g_trk], [otrk],
                      scale=plain_g[:, ch:ch + 1])
        if cb is not None:
            cb(t0, n, o)


def linear_fm(c, wd, kc, G, ngroups, inT, in_trks, tiles, evac, banks=(0, 8)):
    b = c.b
    for g in range(ngroups):
        wt, wtrk = wslot(c)
        b.dma(POOL, wt[:, 0:G * kc * 128], wd[g], writes=[wtrk])
        for gi in range(G):
            f = g * G + gi
            for ti, (t0, n) in enumerate(tiles):
                itrk = in_trks[ti] if isinstance(in_trks, list) else in_trks
                ps, ps_trk = b.nextbank(*banks)
                for k in range(kc):
                    col = (gi * kc + k) * 128
                    b.mm(ps[:, 0:n], wt[:, col:col + 128], inT[:, k, t0:t0 + n], k == 0, k == kc - 1,
                         [wtrk, itrk], [ps_trk], inc=(k == kc - 1))
                evac(f, ti, t0, n, ps, ps_trk)


FFN_SUPERS = [[(0, 512, 0), (512, 512, 0), (1024, 128, 0)], [(1152, 512, 0), (1664, 384, 0), (2048, 256, 1)]]


def ffn(c, ng2_d, w_in_d, w_out_d, supers=None):
    b = c.b
    maxtok = 1152
    with c.arena as es:
        make_AB(c, ng2_d, 1, es)
        hT = b.sb([128, 8, maxtok], BF16, "hT2", es)
        h_trk = Trk()
        gT = b.sb([128, 22, maxtok], BF16, "gT", es)
        g_trk = Trk()
        sa = [(b.sb([128, maxtok], F32, "sa", es), Trk()) for _ in range(2)]
        for s in (supers or FFN_SUPERS):
            offs = []
            o = 0
            for (t0, n, _) in s:
                offs.append(o)
                o += n
            norm_mod(c, hT, h_trk, 1, [(s[i][0], s[i][1], s[i][2], offs[i]) for i in range(len(s))])
            tl = [(offs[i], s[i][1]) for i in range(len(s))]

            def evac1(f, ti, o, n, ps, ps_trk):
                j, isb = f // 2, f % 2
                st, st_trk = sa[j % 2]
                if not isb:
                    b.act(st[:, o:o + n], ps[:, 0:n], AF.Silu, [ps_trk], [st_trk])
                else:
                    b.tt(DVE, gT[:, j, o:o + n], ps[:, 0:n], st[:, o:o + n], ALU.mult, [ps_trk, st_trk], [g_trk])

            linear_fm(c, w_in_d, 8, 2, 22, hT, h_trk, tl, evac1)

            def evac2(f, ti, o, n, ps, ps_trk, s=s):
                t0, _, w = s[ti]
                b.stt(DVE, c.xT[:, f, t0:t0 + n], ps[:, 0:n], c.modT[:, 40 + f, w:w + 1], c.xT[:, f, t0:t0 + n],
                      ALU.mult, ALU.add, [ps_trk, c.mod_trk, c.x_trk], [c.x_trk])

            linear_fm(c, w_out_d, 22, 1, 8, gT, g_trk, tl, evac2)
        b.barrier()


def common_inputs(b):
    d = {}
    d["ada_w"] = b.din("ada_w", [D, 6 * D])
    d["ada_bT"] = b.din("ada_bT", [128, 48, 2])
    d["ccT"] = b.din("ccT", [128, 8, 2])
    d["ng1"] = b.din("ng1", [128, 8])
    d["ng2"] = b.din("ng2", [128, 8])
    d["ffn_w_in"] = b.din("ffn_w_in", [22, 128, 2 * 8 * 128])
    d["ffn_w_out"] = b.din("ffn_w_out", [8, 128, 22 * 128])
    return d


def build_layer1():
    b = Builder()
    TT = NT + NCX + 2
    xT_d = b.din("xT", [128, 8, TT])
    d = common_inputs(b)
    w_in_d = b.din("sc_w_in", [8, 128, 3 * 8 * 128])
    convT_d = b.din("convT", [128, 3, 8])
    hmask_d = b.din("hmask", [128, 2])
    w_out_d = b.din("sc_w_out", [8, 128, 8 * 128])
    xo_d, xo_trk = b.dout("xo", [128, 8, NT + NCX])

    c = setup_common(b, TT)
    for ch in range(8):
        b.dma(SP, c.xT[:, ch, :], xT_d[:, ch, :], writes=[c.x_trk])
    with c.arena as es:
        compute_mods(c, d["ada_w"], d["ada_bT"], d["ccT"], es)
        make_AB(c, d["ng1"], 0, es)
        b.barrier()
    with c.arena as es:
        hT = b.sb([128, 8, TT], BF16, "hT", es)
        h_trk = Trk()
        yT = b.sb([128, 8, NT + NCX], BF16, "yT", es)
        y_trk = Trk()
        bg = b.sb([128, TT], BF16, "bg", es)
        bg_trk = Trk()
        cg = b.sb([128, TT], BF16, "cg", es)
        cg_trk = Trk()
        zp = b.sb([128, NT + 2], F32, "zp", es)
        zc = b.sb([128, NCX + 2], F32, "zc", es)
        z_trk = Trk()
        zh = b.sb([128, 2], F32, "zh", es)
        zh_trk = Trk()
        conv = b.sb([128, 3, 8], F32, "convw", es)
        conv_trk = Trk()
        hm = b.sb([128, 2], F32, "hm", es)
        hm_trk = Trk()
        acc = [(b.sb([128, 512], F32, "acc", es), Trk()) for _ in range(2)]
        b.dma(SP, conv[:], convT_d, writes=[conv_trk])
        b.dma(SP, hm[:], hmask_d, writes=[hm_trk])
        b.memset(DVE, zc[:], 0.0, [z_trk])
        tl = tiles_of(NT) + tiles_of(NCX, NT) + [(NT + NCX, 2)]
        norm_mod(c, hT, h_trk, 0, [(t0, n, (1 if NT <= t0 < NT + NCX else 0), t0) for (t0, n) in tl])
        st = {"ai": 0}

        def evac_all(f, ti, t0, n, ps, ps_trk):
            ch, kind = f // 3, f % 3
            if kind == 0:
                b.op(ACT, lambda e, o=bg[:, t0:t0 + n], i=ps[:, 0:n]: e.copy(o, i), reads=[ps_trk], writes=[bg_trk])
                return
            if kind == 1:
                b.op(ACT, lambda e, o=cg[:, t0:t0 + n], i=ps[:, 0:n]: e.copy(o, i), reads=[ps_trk], writes=[cg_trk])
                return
            if t0 < NT:
                dst = zp[:, 1 + t0:1 + t0 + n]
            elif t0 < NT + NCX:
                dst = zc[:, 1 + t0 - NT:1 + t0 - NT + n]
            else:
                b.tt(DVE, zh[:, 0:2], ps[:, 0:2], cg[:, t0:t0 + 2], ALU.mult, [ps_trk, cg_trk], [zh_trk])
                b.tt(DVE, zp[:, 0:1], zh[:, 0:1], hm[:, 0:1], ALU.mult, [zh_trk, hm_trk], [z_trk])
                b.tt(DVE, zp[:, NT + 1:NT + 2], zh[:, 1:2], hm[:, 1:2], ALU.mult, [zh_trk, hm_trk], [z_trk])
                for (u0, m) in tiles_of(NT) + tiles_of(NCX, NT):
                    a, a_trk = acc[st["ai"] % 2]
                    st["ai"] += 1
                    if u0 < NT:
                        src, o = zp, u0
                    else:
                        src, o = zc, u0 - NT
                    b.ts(DVE, a[:, 0:m], src[:, o:o + m], conv[:, 0, ch:ch + 1], None, ALU.mult, None,
                         [z_trk, conv_trk], [a_trk])
                    b.stt(DVE, a[:, 0:m], src[:, o + 1:o + 1 + m], conv[:, 1, ch:ch + 1], a[:, 0:m],
                          ALU.mult, ALU.add, [z_trk, conv_trk, a_trk], [a_trk])
                    b.stt(DVE, a[:, 0:m], src[:, o + 2:o + 2 + m], conv[:, 2, ch:ch + 1], a[:, 0:m],
                          ALU.mult, ALU.add, [z_trk, conv_trk, a_trk], [a_trk])
                    b.tt(DVE, yT[:, ch, u0:u0 + m], a[:, 0:m], bg[:, u0:u0 + m], ALU.mult, [a_trk, bg_trk], [y_trk])
                return
            b.tt(DVE, dst, ps[:, 0:n], cg[:, t0:t0 + n], ALU.mult, [ps_trk, cg_trk], [z_trk])

        linear_fm(c, w_in_d, 8, 3, 8, hT, h_trk, tl, evac_all)
        tl2 = tiles_of(NT) + tiles_of(NCX, NT)

        def evac_out(f, ti, t0, n, ps, ps_trk):
            w = 0 if t0 < NT else 1
            b.stt(DVE, c.xT[:, f, t0:t0 + n], ps[:, 0:n], c.modT[:, 16 + f, w:w + 1], c.xT[:, f, t0:t0 + n],
                  ALU.mult, ALU.add, [ps_trk, c.mod_trk, c.x_trk], [c.x_trk])

        linear_fm(c, w_out_d, 8, 1, 8, yT, y_trk, tl2, evac_out)
        b.barrier()
    ffn(c, d["ng2"], d["ffn_w_in"], d["ffn_w_out"])
    for ch in range(8):
        b.dma(SP, xo_d[:, ch, :], c.xT[:, ch, 0:NT + NCX], reads=[c.x_trk], writes=[xo_trk], st=c.x_trk)
    return b.finish()


def layer1_inputs(xs, ctx, inp, i=1):
    sc_w_in = inp["sc_w_in"][0]
    cols = []
    for ch in range(8):
        cols += list(range(ch * 128, (ch + 1) * 128))
        cols += list(range(1024 + ch * 128, 1024 + (ch + 1) * 128))
        cols += list(range(2048 + ch * 128, 2048 + (ch + 1) * 128))
    w_in = lhsT_blocks(sc_w_in[:, cols], 3)
    common = {
        "ada_w": np.ascontiguousarray(inp["ada_w"][i]),
        "ada_bT": np.ascontiguousarray(np.repeat(fm(inp["ada_b"][i])[:, :, None], 2, axis=2)),
        "ccT": np.ascontiguousarray(np.stack([fm(inp["c"][0]), fm(inp["c_ctx"])], axis=-1)),
        "ng1": fm(inp["norm1_g"][i]),
        "ng2": fm(inp["norm2_g"][i]),
        "sc_w_in": w_in,
        "convT": fm(inp["sc_conv_w"][0]),
        "sc_w_out": lhsT_blocks(inp["sc_w_out"][0], 1),
        "ffn_w_in": lhsT_blocks(ffn_in_cols(inp["ffn_w_in"][i]), 2),
        "ffn_w_out": lhsT_blocks(inp["ffn_w_out"][i], 1),
    }
    maps = []
    z = np.zeros((1, D), np.float32)
    for cid in range(NCORES):
        lo, hi = cid * NT, (cid + 1) * NT
        left = xs[lo - 1:lo] if cid > 0 else z
        right = xs[hi:hi + 1] if cid < NCORES - 1 else z
        xt = np.concatenate([xs[lo:hi], ctx, left, right], axis=0)
        m = dict(common)
        m["xT"] = toT(xt)
        hm = np.ones((128, 2), np.float32)
        if cid == 0:
            hm[:, 0] = 0
        if cid == NCORES - 1:
            hm[:, 1] = 0
        m["hmask"] = hm
        maps.append(m)
    return maps


def ffn_in_cols(w):
    cols = []
    for j in range(22):
        cols += list(range(j * 128, (j + 1) * 128))
        cols += list(range(DFF + j * 128, DFF + (j + 1) * 128))
    return w[:, cols]


def run_layer1(xs, ctx, inp):
    nc = build_layer1()
    maps = layer1_inputs(xs, ctx, inp)
    res = run_bass_kernel_spmd(nc, maps, core_ids=list(range(NCORES)))
    xo = [fromT(r["xo"]) for r in res.results]
    x_new = np.concatenate([o[:NT] for o in xo], axis=0)
    ctx_new = xo[0][NT:]
    return x_new, ctx_new


def rope_tables_T(pos):
    pos = np.asarray(pos)
    inv = (1.0 / (np.float32(10000.0) ** (np.arange(0, 32, 2, dtype=np.float32) / np.float32(32.0)))).astype(np.float32)
    pp = np.maximum(pos, 0)
    row = (pp // 64).astype(np.float32)
    col = (pp % 64).astype(np.float32)
    ang = np.concatenate([row[:, None] * inv[None, :], col[:, None] * inv[None, :]], axis=-1).astype(np.float32)
    cos = np.cos(ang).astype(np.float32)
    sin = np.sin(ang).astype(np.float32)
    cos[pos < 0] = 1.0
    sin[pos < 0] = 0.0
    C = np.zeros((128, len(pos)), np.float32)
    S = np.zeros((128, len(pos)), np.float32)
    for hh in range(2):
        for a in range(2):
            for half in range(2):
                p0 = hh * 64 + a * 32 + half * 16
                C[p0:p0 + 16] = cos[:, a * 16:(a + 1) * 16].T
                S[p0:p0 + 16] = (sin[:, a * 16:(a + 1) * 16].T) * (-1.0 if half == 0 else 1.0)
    return C, S


def swap_cols(W):
    K, F = W.shape
    a = W.reshape(K, F // 64, 2, 2, 16)
    return np.ascontiguousarray(a[:, :, :, ::-1, :].reshape(K, F))


def kcol2(t0):
    if t0 < NT:
        return 128 + t0
    if t0 < NT + NCX:
        return 2304 + (t0 - NT)
    if t0 < NT + NCX + 128:
        return 0
    return 2176


def build_layer2():
    b = Builder()
    TT = NT + NCX + 256
    NQ = NT + NCX
    xT_d = b.din("xT", [128, 8, TT])
    d = common_inputs(b)
    wqk_d = b.din("wqk", [12, 128, 2048])
    wv_d = b.din("wv", [128, 8 * 256])
    ropeC_d = b.din("ropeC", [128, TT])
    ropeS_d = b.din("ropeS", [128, TT])
    masks_d = b.din("masks", [128, 8, 512])
    sink_d = b.din("sinks", [128, 16])
    wo_d = b.din("wo", [8, 128, 1024])
    xo_d, xo_trk = b.dout("xo", [128, 8, NQ])
    q_scr = b.dscr("q_scr", [8, 128, NQ])
    k_scr = b.dscr("k_scr", [4, 128, TT])
    v_scr = b.dscr("v_scr", [20, 128, 512])
    scr_trk = Trk()

    c = setup_common(b, TT, 23000)
    for ch in range(8):
        b.dma(SP, c.xT[:, ch, :], xT_d[:, ch, :], writes=[c.x_trk])
    with c.arena as es:
        compute_mods(c, d["ada_w"], d["ada_bT"], d["ccT"], es)
        make_AB(c, d["ng1"], 0, es)
        b.barrier()
    tl_q = tiles_of(NT) + tiles_of(NCX, NT)
    tl_all = tl_q + [(NQ, 128), (NQ + 128, 128)]
    with c.arena as es:
        hT = b.sb([128, 8, TT], BF16, "hT", es)
        h_trk = Trk()
        C = b.sb([128, TT], F32, "C", es)
        S = b.sb([128, TT], F32, "S", es)
        cs_trk = Trk()
        rt = b.sb([128, TT], F32, "rt", es)
        rt_trk = Trk()
        stg = [(b.sb([128, 512], BF16, "stg", es), Trk()) for _ in range(3)]
        vst = [(b.sb([128, 4, 128], BF16, "vst", es), Trk()) for _ in range(2)]
        st = {"i": 0, "v": 0}
        b.dma(SP, C[:], ropeC_d, writes=[cs_trk])
        b.dma(SP, S[:], ropeS_d, writes=[cs_trk])
        norm_mod(c, hT, h_trk, 0, [(t0, n, (1 if NT <= t0 < NQ else 0), t0) for (t0, n) in tl_all])

        def mk_evac(isk):
            def evac(f, ti, t0, n, ps, ps_trk):
                grp, sw = f // 2, f % 2
                if not sw:
                    b.tt(DVE, rt[:, t0:t0 + n], ps[:, 0:n], C[:, t0:t0 + n], ALU.mult, [ps_trk, cs_trk], [rt_trk])
                    return
                tmp, tmp_trk = c.tmp[c.tmpi % 2]
                c.tmpi += 1
                b.tt(DVE, tmp[:, 0:n], ps[:, 0:n], S[:, t0:t0 + n], ALU.mult, [ps_trk, cs_trk], [tmp_trk])
                sg, sg_trk = stg[st["i"] % 3]
                st["i"] += 1
                b.tt(DVE, sg[:, 0:n], tmp[:, 0:n], rt[:, t0:t0 + n], ALU.add, [tmp_trk, rt_trk], [sg_trk])
                if not isk:
                    b.dma(SP, q_scr[grp, :, t0:t0 + n], sg[:, 0:n], reads=[sg_trk], writes=[scr_trk], st=sg_trk)
                else:
                    kc0 = kcol2(t0)
                    b.dma(SP, k_scr[grp, :, kc0:kc0 + n], sg[:, 0:n], reads=[sg_trk], writes=[scr_trk], st=sg_trk)
            return evac

        linear_fm(c, wqk_d[0:8], 8, 2, 8, hT, h_trk, tl_q, mk_evac(False))
        linear_fm(c, wqk_d[8:12], 8, 2, 4, hT, h_trk, tl_all, mk_evac(True))
        wt, wtrk = wslot(c)
        b.dma(POOL, wt[:, 0:2048], wv_d, writes=[wtrk])
        for (t0, n) in tl_all:
            for u in range(n // 128):
                tk = t0 + u * 128
                kt = kcol2(t0) // 128 + u
                ps, ps_trk = b.nextbank()
                for k in range(8):
                    b.mm(ps[:, 0:256], hT[:, k, tk:tk + 128], wt[:, k * 256:(k + 1) * 256], k == 0, k == 7,
                         [h_trk, wtrk], [ps_trk], inc=(k == 7))
                vs, vs_trk = vst[st["v"] % 2]
                st["v"] += 1
                src = ps[:, 0:256].rearrange("p (g e) -> p g e", g=4)
                b.copy(DVE, vs[:, :, 0:64], src, [ps_trk], [vs_trk])
                b.copy(DVE, vs[:, :, 64:128], src, [ps_trk], [vs_trk])
                b.dma(SP, v_scr[kt], vs[:].rearrange("p g e -> p (g e)"), reads=[vs_trk], writes=[scr_trk], st=vs_trk)
        b.barrier()
    with c.arena as es:
        oT = b.sb([128, 8, NQ], BF16, "oT", es)
        o_trk = Trk()
        masks = b.sb([128, 8, 512], BF16, "masks", es)
        m_trk = Trk()
        sink = b.sb([128, 16], F32, "sink", es)
        sk_trk = Trk()
        kTr = [(b.sb([128, TT], BF16, "kT", es), Trk()) for _ in range(2)]
        vTr = [(b.sb([128, 20, 128], BF16, "vT", es), Trk()) for _ in range(2)]
        qTr = [(b.sb([128, NQ], BF16, "qT", es), Trk()) for _ in range(2)]
        pTr = [(b.sb([128, 512], BF16, "pT", es), Trk()) for _ in range(3)]
        rdr = [(b.sb([128, 512], F32, "rd", es), Trk()) for _ in range(2)]
        b.dma(POOL, masks[:], masks_d, writes=[m_trk])
        b.dma(SP, sink[:], sink_d, writes=[sk_trk])
        b.act(sink[:], sink[:], AF.Exp, [sk_trk], [sk_trk])
        cnt = {"p": 0, "s": 0, "u": 0, "q": 0}
        for g in range(4):
            kT, k_trk = kTr[g % 2]
            vT, v_trk = vTr[g % 2]
            b.dma(SP, kT[:], k_scr[g], reads=[scr_trk], writes=[k_trk])
            for t4 in range(0, 20, 4):
                b.dma(SP, vT[:, t4:t4 + 4, :], v_scr[t4:t4 + 4, :, g * 128:(g + 1) * 128].rearrange("t p e -> p t e"),
                      reads=[scr_trk], writes=[v_trk])
            for r in range(4):
                h = g * 4 + r
                qc, half = h // 2, h % 2
                if r % 2 == 0:
                    qT, q_trk = qTr[cnt["q"] % 2]
                    cnt["q"] += 1
                    b.dma(SP, qT[:], q_scr[qc], reads=[scr_trk], writes=[q_trk])
                rows = slice(half * 64, half * 64 + 64)
                units = []
                for qt in range(4):
                    keys = [(18, 0, 512, None), (19, 0, 512, None)]
                    for j in range(6):
                        mi = j
                        if qt == 0 and j == 0:
                            mi = 6
                        if qt == 3 and j == 5:
                            mi = 7
                        keys.append((4 * qt + j, max(0, j - 2) * 128, (min(3, j) + 1) * 128, mi))
                    units.append((qt * 512, 512, keys))
                units.append((NT, NCX, [(18, 0, 256, None), (19, 0, 256, None)]))
                for (q0, nq, keys) in units:
                    ob, ob_trk = b.bank(4 + cnt["u"] % 2)
                    db, db_trk = b.bank(6 + cnt["u"] % 2)
                    cnt["u"] += 1
                    def emit_S2(idx, keys=keys, q0=q0, rows=rows, kT=kT, k_trk=k_trk, qT=qT, q_trk=q_trk):
                        kt, qlo, qhi, mi = keys[idx]
                        nn = qhi - qlo
                        sb_, sb_trk = b.bank(cnt["s"] % 4)
                        cnt["s"] += 1
                        b.mm(sb_[:, 0:nn], kT[rows, kt * 128:(kt + 1) * 128], qT[rows, q0 + qlo:q0 + qhi], True, True,
                             [k_trk, q_trk], [sb_trk], inc=True)
                        return sb_, sb_trk

                    s_info = {0: emit_S2(0)}
                    for idx, (kt, qlo, qhi, mi) in enumerate(keys):
                        nn = qhi - qlo
                        if idx + 1 < len(keys):
                            s_info[idx + 1] = emit_S2(idx + 1)
                        sb_, sb_trk = s_info.pop(idx)
                        pT, p_trk = pTr[cnt["p"] % 3]
                        cnt["p"] += 1
                        b.act(pT[:, 0:nn], sb_[:, 0:nn], AF.Exp, [sb_trk], [p_trk], scale=0.125)
                        if mi is not None:
                            b.tt(DVE, pT[:, 0:nn], pT[:, 0:nn], masks[:, mi, qlo:qhi], ALU.mult, [p_trk, m_trk], [p_trk])
                        last = idx == len(keys) - 1
                        b.mm(ob[:, qlo:qhi], vT[:, kt, :], pT[:, 0:nn], idx == 0, last, [v_trk, p_trk], [ob_trk],
                             inc=False)
                        b.mm(db[:, qlo:qhi], c.ones[:], pT[:, 0:nn], idx == 0, last, [c.ones_trk, p_trk], [db_trk],
                             inc=True)
                    rd, rd_trk = rdr[cnt["u"] % 2]
                    b.ts(DVE, rd[:, 0:nq], db[:, 0:nq], sink[:, h:h + 1], None, ALU.add, None, [db_trk, sk_trk], [rd_trk])
                    b.recip(rd[:, 0:nq], rd[:, 0:nq], [rd_trk], [rd_trk])
                    b.tt(DVE, oT[rows, qc, q0:q0 + nq], ob[rows, 0:nq], rd[rows, 0:nq], ALU.mult, [ob_trk, rd_trk],
                         [o_trk])

        def evac_out(f, ti, t0, n, ps, ps_trk):
            w = 0 if t0 < NT else 1
            b.stt(DVE, c.xT[:, f, t0:t0 + n], ps[:, 0:n], c.modT[:, 16 + f, w:w + 1], c.xT[:, f, t0:t0 + n],
                  ALU.mult, ALU.add, [ps_trk, c.mod_trk, c.x_trk], [c.x_trk])

        linear_fm(c, wo_d, 8, 1, 8, oT, o_trk, tl_q, evac_out, banks=(0, 4))
        b.barrier()
    ffn(c, d["ng2"], d["ffn_w_in"], d["ffn_w_out"])
    for ch in range(8):
        b.dma(SP, xo_d[:, ch, :], c.xT[:, ch, 0:NQ], reads=[c.x_trk], writes=[xo_trk], st=c.x_trk)
    return b.finish()


def common_host(inp, i):
    return {
        "ada_w": np.ascontiguousarray(inp["ada_w"][i]),
        "ada_bT": np.ascontiguousarray(np.repeat(fm(inp["ada_b"][i])[:, :, None], 2, axis=2)),
        "ccT": np.ascontiguousarray(np.stack([fm(inp["c"][0]), fm(inp["c_ctx"])], axis=-1)),
        "ng1": fm(inp["norm1_g"][i]),
        "ng2": fm(inp["norm2_g"][i]),
        "ffn_w_in": lhsT_blocks(ffn_in_cols(inp["ffn_w_in"][i]), 2),
        "ffn_w_out": lhsT_blocks(inp["ffn_w_out"][i], 1),
    }


def layer2_inputs(xs, ctx, inp, i=2):
    W = inp["wa_wqkv"][0]
    Wq, Wk, Wv = W[:, 0:1024], W[:, 1024:1280], W[:, 1280:1536]
    Wqs, Wks = swap_cols(Wq), swap_cols(Wk)
    blocks = []
    for qc in range(8):
        blocks += [Wq[:, qc * 128:(qc + 1) * 128], Wqs[:, qc * 128:(qc + 1) * 128]]
    for g in range(4):
        kg, kgs = Wk[:, g * 64:(g + 1) * 64], Wks[:, g * 64:(g + 1) * 64]
        blocks += [np.concatenate([kg, kg], axis=1), np.concatenate([kgs, kgs], axis=1)]
    wqk = lhsT_blocks(np.concatenate(blocks, axis=1), 2)
    common = common_host(inp, i)
    common.update({
        "wqk": wqk,
        "wv": np.ascontiguousarray(Wv.reshape(8, 128, 256).transpose(1, 0, 2).reshape(128, 2048)),
        "sinks": np.ascontiguousarray(np.broadcast_to(inp["wa_sinks"][0][None, :], (128, 16))),
        "wo": lhsT_blocks(inp["wa_wo"][0], 1),
    })
    k = np.arange(128)[:, None]
    q = np.arange(512)[None, :]
    base_masks = [(np.abs((j - 1) * 128 + k - q) <= 128).astype(np.float32) for j in range(6)]
    maps = []
    z = np.zeros((128, D), np.float32)
    for cid in range(NCORES):
        lo, hi = cid * NT, (cid + 1) * NT
        left = xs[lo - 128:lo] if cid > 0 else z
        right = xs[hi:hi + 128] if cid < NCORES - 1 else z
        xt = np.concatenate([xs[lo:hi], ctx, left, right], axis=0)
        pos = np.concatenate([np.arange(lo, hi), -np.ones(NCX, np.int64), np.arange(lo - 128, lo), np.arange(hi, hi + 128)])
        pos = np.where((pos >= SEQ), 0, pos)
        if cid == 0:
            pos[NT + NCX:NT + NCX + 128] = 0
        Ct, St = rope_tables_T(pos)
        m = dict(common)
        m["xT"] = toT(xt)
        m["ropeC"], m["ropeS"] = Ct, St
        mk = base_masks + [base_masks[0] * (0.0 if cid == 0 else 1.0), base_masks[5] * (0.0 if cid == NCORES - 1 else 1.0)]
        m["masks"] = np.ascontiguousarray(np.stack(mk, axis=1))
        maps.append(m)
    return maps


def run_layer2(xs, ctx, inp):
    nc = build_layer2()
    maps = layer2_inputs(xs, ctx, inp)
    res = run_bass_kernel_spmd(nc, maps, core_ids=list(range(NCORES)))
    xo = [fromT(r["xo"]) for r in res.results]
    return np.concatenate([o[:NT] for o in xo], axis=0), xo[0][NT:]


LAM_INIT0 = 0.8 - 0.6 * math.exp(-0.3 * 0)
NKT = (SEQ + NCX) // 128
KCH = 13


def build_A0():
    b = Builder()
    TT = NT + NCX
    xT_d = b.din("xT", [128, 8, TT])
    d = common_inputs(b)
    wqk_d = b.din("wqk", [16, 128, 2048])
    wv_d = b.din("wv", [128, 8 * 1024])
    ropeC_d = b.din("ropeC", [128, TT])
    ropeS_d = b.din("ropeS", [128, TT])
    q_o, q_trk = b.dout("qT_o", [8, 128, TT], BF16)
    k_o, k_trk = b.dout("kT_o", [8, 128, TT], BF16)
    v_o, v_trk = b.dout("v_o", [18, 128, 1024], BF16)
    mod_o, mod_trk = b.dout("modT_o", [128, 96])
    c = setup_common(b, TT)
    for ch in range(8):
        b.dma(SP, c.xT[:, ch, :], xT_d[:, ch, :], writes=[c.x_trk])
    with c.arena as es:
        compute_mods(c, d["ada_w"], d["ada_bT"], d["ccT"], es)
        make_AB(c, d["ng1"], 0, es)
        b.dma(SP, mod_o, c.modT[:].rearrange("p j w -> p (j w)"), reads=[c.mod_trk], writes=[mod_trk], st=c.mod_trk)
        b.barrier()
    tl = tiles_of(NT) + tiles_of(NCX, NT)
    with c.arena as es:
        hT = b.sb([128, 8, TT], BF16, "hT", es)
        h_trk = Trk()
        C = b.sb([128, TT], F32, "C", es)
        S = b.sb([128, TT], F32, "S", es)
        cs_trk = Trk()
        rt = b.sb([128, TT], F32, "rt", es)
        rt_trk = Trk()
        wv = b.sb([128, 8 * 1024], BF16, "wv", es)
        wv_trk = Trk()
        stg = [(b.sb([128, 512], BF16, "stg", es), Trk()) for _ in range(3)]
        vst = [(b.sb([128, 1024], BF16, "vst", es), Trk()) for _ in range(2)]
        st = {"i": 0, "v": 0}
        b.dma(SP, C[:], ropeC_d, writes=[cs_trk])
        b.dma(SP, S[:], ropeS_d, writes=[cs_trk])
        for k in range(8):
            b.dma(POOL, wv[:, k * 1024:(k + 1) * 1024], wv_d[:, k * 1024:(k + 1) * 1024], writes=[wv_trk])
        norm_mod(c, hT, h_trk, 0, [(t0, n, (1 if t0 >= NT else 0), t0) for (t0, n) in tl])

        def evac(f, ti, t0, n, ps, ps_trk):
            grp, sw = f // 2, f % 2
            if not sw:
                b.tt(DVE, rt[:, t0:t0 + n], ps[:, 0:n], C[:, t0:t0 + n], ALU.mult, [ps_trk, cs_trk], [rt_trk])
                return
            tmp, tmp_trk = c.tmp[c.tmpi % 2]
            c.tmpi += 1
            b.tt(DVE, tmp[:, 0:n], ps[:, 0:n], S[:, t0:t0 + n], ALU.mult, [ps_trk, cs_trk], [tmp_trk])
            sg, sg_trk = stg[st["i"] % 3]
            st["i"] += 1
            b.tt(DVE, sg[:, 0:n], tmp[:, 0:n], rt[:, t0:t0 + n], ALU.add, [tmp_trk, rt_trk], [sg_trk])
            if grp < 8:
                b.dma(SP, q_o[grp, :, t0:t0 + n], sg[:, 0:n], reads=[sg_trk], writes=[q_trk], st=sg_trk)
            else:
                b.dma(SP, k_o[grp - 8, :, t0:t0 + n], sg[:, 0:n], reads=[sg_trk], writes=[k_trk], st=sg_trk)

        linear_fm(c, wqk_d, 8, 2, 16, hT, h_trk, tl, evac)
        for kt in range(18):
            tk = kt * 128
            vs, vs_trk = vst[st["v"] % 2]
            st["v"] += 1
            for hf in range(2):
                ps, ps_trk = b.nextbank()
                for k in range(8):
                    b.mm(ps[:, 0:512], hT[:, k, tk:tk + 128], wv[:, k * 1024 + hf * 512:k * 1024 + hf * 512 + 512],
                         k == 0, k == 7, [h_trk, wv_trk], [ps_trk], inc=(k == 7))
                if hf == 0:
                    b.copy(DVE, vs[:, 0:512], ps[:, 0:512], [ps_trk], [vs_trk])
                else:
                    b.op(ACT, lambda e, o=vs[:, 512:1024], i=ps[:, 0:512]: e.copy(o, i), reads=[ps_trk], writes=[vs_trk])
            b.dma(SP, v_o[kt], vs[:], reads=[vs_trk], writes=[v_trk], st=vs_trk)
        b.barrier()
    return b.finish()


def build_B0():
    b = Builder()
    TT = NT + NCX
    xT_d = b.din("xT", [128, 8, TT])
    d = common_inputs(b)
    mod_d = b.din("modT", [128, 96])
    q_d = b.din("qT", [8, 128, TT], BF16)
    k_d = b.din("kT_all", [8, 128, NKT * 128], BF16)
    v_d = b.din("v_all", [NKT, 128, 1024], BF16)
    lam_d = b.din("lamb", [128, 4, 64])
    gs_d = b.din("gsub", [128, 1])
    wo_d = b.din("wo", [8, 128, 1024])
    xo_d, xo_trk = b.dout("xo", [128, 8, TT])
    c = setup_common(b, TT)
    for ch in range(8):
        b.dma(SP, c.xT[:, ch, :], xT_d[:, ch, :], writes=[c.x_trk])
    b.dma(SP, c.modT[:].rearrange("p j w -> p (j w)"), mod_d, writes=[c.mod_trk])
    tl = tiles_of(NT) + tiles_of(NCX, NT)
    with c.arena as es:
        oT = b.sb([128, 8, TT], BF16, "oT", es)
        o_trk = Trk()
        qTr = [(b.sb([128, TT], BF16, "qT", es), Trk()) for _ in range(2)]
        kcr = [(b.sb([128, KCH * 128], BF16, "kch", es), Trk()) for _ in range(3)]
        vcr = [(b.sb([128, KCH, 128], BF16, "vch", es), Trk()) for _ in range(3)]
        pTr = [(b.sb([128, 2, 512], BF16, "pT", es), Trk()) for _ in range(3)]
        t1 = b.sb([128, 512], F32, "t1", es)
        t2 = b.sb([128, 512], F32, "t2", es)
        r1 = b.sb([128, 512], F32, "r1", es)
        r2 = b.sb([128, 512], F32, "r2", es)
        f_trk = Trk()
        sqb = b.sb([128, 512], BF16, "sqb", es)
        sq_trk = Trk()
        lamb = b.sb([128, 4, 64], F32, "lamb", es)
        lam_trk = Trk()
        lj = b.sb([128, 2, 64], F32, "lj", es)
        ls = b.sb([128, 4], F32, "ls", es)
        gs = b.sb([128, 1], F32, "gs", es)
        gs_trk = Trk()
        b.dma(SP, lamb[:], lam_d, writes=[lam_trk])
        b.dma(SP, gs[:], gs_d, writes=[gs_trk])
        acc2 = b.sb([128, 512], F32, "acc2", es)
        acc_trk = Trk()
        onesf = b.sb([128, 128], F32, "onesf", es)
        onesf_trk = Trk()
        b.memset(DVE, onesf[:], 1.0, [onesf_trk])
        b.memset(DVE, ls[:], 0.0, [lam_trk])
        b.tt(DVE, lj[:, 0, :], lamb[:, 0, :], lamb[:, 1, :], ALU.mult, [lam_trk], [lam_trk])
        b.tt(DVE, lj[:, 1, :], lamb[:, 2, :], lamb[:, 3, :], ALU.mult, [lam_trk], [lam_trk])
        b.act(lj[:, 0, :], lj[:, 0, :], AF.Copy, [lam_trk], [lam_trk], accum_out=ls[:, 0:1])
        b.act(lj[:, 1, :], lj[:, 1, :], AF.Copy, [lam_trk], [lam_trk], accum_out=ls[:, 1:2])
        b.act(ls[:, 0:2], ls[:, 0:2], AF.Exp, [lam_trk], [lam_trk])
        b.tt(DVE, ls[:, 2:3], ls[:, 1:2], ls[:, 0:1], ALU.subtract, [lam_trk], [lam_trk])
        b.ts(DVE, ls[:, 2:3], ls[:, 2:3], -LAM_INIT0, None, ALU.add, None, [lam_trk], [lam_trk])
        b.ts(DVE, gs[:], gs[:], 1.0 - LAM_INIT0, None, ALU.mult, None, [gs_trk], [gs_trk])
        cnt = {"p": 0, "s": 0, "k": 0, "q": 0}
        ob1, ob1_trk = b.bank(4)
        ob2, ob2_trk = b.bank(5)
        db1, db1_trk = b.bank(6)
        db2, db2_trk = b.bank(7)
        for h in range(8):
            qT, q_trk = qTr[h % 2]
            b.dma(SP, qT[:], q_d[h], writes=[q_trk])
            units = [(qt * 512, 512, NKT) for qt in range(4)] + [(NT, NCX, 2)]
            for (q0, nq, nkt) in units:
                nch = (nkt + KCH - 1) // KCH
                tiles_ = []
                for ch in range(nch):
                    for u in range(min(KCH, nkt - ch * KCH)):
                        tiles_.append((ch, u, ch * KCH + u))
                chunks = {}

                def emit_S(i, tiles_=tiles_, chunks=chunks, h=h, q0=q0, nq=nq, nkt=nkt, qT=qT, q_trk=q_trk):
                    ch, u, kt = tiles_[i]
                    if ch not in chunks:
                        kt0 = ch * KCH
                        nk = min(KCH, nkt - kt0)
                        kc_, kc_trk = kcr[cnt["k"] % 3]
                        vc_, vc_trk = vcr[cnt["k"] % 3]
                        cnt["k"] += 1
                        b.dma(SP, kc_[:, 0:nk * 128], k_d[h, :, kt0 * 128:(kt0 + nk) * 128], writes=[kc_trk])
                        for v0 in range(0, nk, 7):
                            v1 = min(nk, v0 + 7)
                            b.dma(SP, vc_[:, v0:v1, :],
                                  v_d[kt0 + v0:kt0 + v1, :, h * 128:(h + 1) * 128].rearrange("t p e -> p t e"),
                                  writes=[vc_trk])
                        chunks[ch] = (kc_, kc_trk, vc_, vc_trk)
                    kc_, kc_trk, vc_, vc_trk = chunks[ch]
                    si = cnt["s"] % 2
                    cnt["s"] += 1
                    spair = b.pst[si]
                    s_trks = [b.bank_trk[2 * si], b.bank_trk[2 * si + 1]]
                    b.mm(spair[:, 0:nq], kc_[0:64, u * 128:(u + 1) * 128], qT[0:64, q0:q0 + nq], True, True,
                         [kc_trk, q_trk], s_trks, inc=False)
                    b.mm(spair[:, 512:512 + nq], kc_[64:128, u * 128:(u + 1) * 128], qT[64:128, q0:q0 + nq], True,
                         True, [kc_trk, q_trk], s_trks, inc=True)
                    return spair, s_trks

                s_info = {0: emit_S(0)}
                for i, (ch, u, kt) in enumerate(tiles_):
                    if i + 1 < len(tiles_):
                        s_info[i + 1] = emit_S(i + 1)
                    spair, s_trks = s_info.pop(i)
                    kc_, kc_trk, vc_, vc_trk = chunks[ch]
                    pT, p_trk = pTr[cnt["p"] % 3]
                    cnt["p"] += 1
                    b.act(pT[:, :, 0:nq], spair[:].rearrange("p (m q) -> p m q", m=2)[:, :, 0:nq], AF.Exp, s_trks,
                          [p_trk], scale=0.125)
                    first, last = kt == 0, kt == nkt - 1
                    b.mm(ob1[:, 0:nq], vc_[:, u, :], pT[:, 0, 0:nq], first, last, [vc_trk, p_trk], [ob1_trk], inc=False)
                    b.mm(ob2[:, 0:nq], vc_[:, u, :], pT[:, 1, 0:nq], first, last, [vc_trk, p_trk], [ob2_trk], inc=False)
                    b.mm(db1[:, 0:nq], c.ones[:], pT[:, 0, 0:nq], first, last, [c.ones_trk, p_trk], [db1_trk], inc=True)
                    if first:
                        b.copy(DVE, acc2[:, 0:nq], pT[:, 1, 0:nq], [p_trk], [acc_trk])
                    else:
                        b.tt(DVE, acc2[:, 0:nq], acc2[:, 0:nq], pT[:, 1, 0:nq], ALU.add, [p_trk, acc_trk], [acc_trk])
                b.mm(db2[:, 0:nq], onesf[:], acc2[:, 0:nq], True, True, [onesf_trk, acc_trk], [db2_trk], inc=True)
                b.recip(r1[:, 0:nq], db1[:, 0:nq], [db1_trk], [f_trk])
                b.recip(r2[:, 0:nq], db2[:, 0:nq], [db2_trk], [f_trk])
                b.tt(DVE, t1[:, 0:nq], ob1[:, 0:nq], r1[:, 0:nq], ALU.mult, [ob1_trk, f_trk], [f_trk])
                b.tt(DVE, t2[:, 0:nq], ob2[:, 0:nq], r2[:, 0:nq], ALU.mult, [ob2_trk, f_trk], [f_trk])
                b.stt(DVE, t1[:, 0:nq], t2[:, 0:nq], ls[:, 2:3], t1[:, 0:nq], ALU.mult, ALU.add, [f_trk, lam_trk], [f_trk])
                b.act(sqb[:, 0:nq], t1[:, 0:nq], AF.Square, [f_trk], [sq_trk])
                si = cnt["s"] % 2
                cnt["s"] += 1
                ssb = b.pst[si][:, 0:512]
                ss_trks = [b.bank_trk[2 * si], b.bank_trk[2 * si + 1]]
                b.mm(ssb[:, 0:nq], c.ones[:], sqb[:, 0:nq], True, True, [c.ones_trk, sq_trk], ss_trks, inc=True)
                b.act(r1[:, 0:nq], ssb[:, 0:nq], AF.Sqrt, ss_trks + [c.eps_trk], [f_trk], bias=c.eps[:, 0:1], scale=1.0 / 128)
                b.recip(r1[:, 0:nq], r1[:, 0:nq], [f_trk], [f_trk])
                b.tt(DVE, t1[:, 0:nq], t1[:, 0:nq], r1[:, 0:nq], ALU.mult, [f_trk], [f_trk])
                b.act(oT[:, h, q0:q0 + nq], t1[:, 0:nq], AF.Copy, [f_trk, gs_trk], [o_trk], scale=gs[:, 0:1])

        def evac_out(f, ti, t0, n, ps, ps_trk):
            w = 0 if t0 < NT else 1
            b.stt(DVE, c.xT[:, f, t0:t0 + n], ps[:, 0:n], c.modT[:, 16 + f, w:w + 1], c.xT[:, f, t0:t0 + n],
                  ALU.mult, ALU.add, [ps_trk, c.mod_trk, c.x_trk], [c.x_trk])

        linear_fm(c, wo_d, 8, 1, 8, oT, o_trk, tl, evac_out, banks=(0, 4))
        b.barrier()
    ffn(c, d["ng2"], d["ffn_w_in"], d["ffn_w_out"])
    for ch in range(8):
        b.dma(SP, xo_d[:, ch, :], c.xT[:, ch, :], reads=[c.x_trk], writes=[xo_trk], st=c.x_trk)
    return b.finish()


def run_layer0(xs, ctx, inp):
    W = inp["da_wqkv"][0]
    Wq, Wk, Wv = W[:, 0:1024], W[:, 1024:2048], W[:, 2048:3072]
    Wqs, Wks = swap_cols(Wq), swap_cols(Wk)
    blocks = []
    for hh in range(8):
        blocks += [Wq[:, hh * 128:(hh + 1) * 128], Wqs[:, hh * 128:(hh + 1) * 128]]
    for hh in range(8):
        blocks += [Wk[:, hh * 128:(hh + 1) * 128], Wks[:, hh * 128:(hh + 1) * 128]]
    common = common_host(inp, 0)
    a_common = dict(common)
    a_common.update({
        "wqk": lhsT_blocks(np.concatenate(blocks, axis=1), 2),
        "wv": np.ascontiguousarray(Wv.reshape(8, 128, 1024).transpose(1, 0, 2).reshape(128, 8192)),
    })
    maps = []
    for cid in range(NCORES):
        lo, hi = cid * NT, (cid + 1) * NT
        pos = np.concatenate([np.arange(lo, hi), -np.ones(NCX, np.int64)])
        Ct, St = rope_tables_T(pos)
        m = dict(a_common)
        m["xT"] = toT(np.concatenate([xs[lo:hi], ctx], axis=0))
        m["ropeC"], m["ropeS"] = Ct, St
        maps.append(m)
    resA = run_bass_kernel_spmd(build_A0(), maps, core_ids=list(range(NCORES))).results
    kT_all = np.concatenate([resA[0]["kT_o"][:, :, NT:]] + [r["kT_o"][:, :, 0:NT] for r in resA], axis=2)
    v_all = np.concatenate([resA[0]["v_o"][16:18]] + [r["v_o"][0:16] for r in resA], axis=0)
    kT_all = np.ascontiguousarray(kT_all)
    v_all = np.ascontiguousarray(v_all)
    b_common = dict(common)
    b_common.update({
        "kT_all": kT_all, "v_all": v_all,
        "lamb": np.ascontiguousarray(np.broadcast_to(inp["da_lambda"][0][None], (128, 4, 64))),
        "gsub": np.ascontiguousarray(inp["da_subln_g"][0].reshape(128, 1)),
        "wo": lhsT_blocks(inp["da_wo"][0], 1),
    })
    mapsB = []
    for cid in range(NCORES):
        m = dict(b_common)
        m["xT"] = maps[cid]["xT"]
        m["modT"] = resA[cid]["modT_o"]
        m["qT"] = resA[cid]["qT_o"]
        mapsB.append(m)
    resB = run_bass_kernel_spmd(build_B0(), mapsB, core_ids=list(range(NCORES))).results
    xo = [fromT(r["xo"]) for r in resB]
    return np.concatenate([o[:NT] for o in xo], axis=0), xo[0][NT:]


def hg_consts(c, es, b):
    k = Ctx()
    k.trif = b.sb([128, 2, 128], F32, "trif", es)
    k.onesf = b.sb([128, 128], F32, "onesf", es)
    k.maskb = b.sb([128, 2, 128], BF16, "maskb", es)
    k.ident = b.sb([128, 128], BF16, "ident", es)
    k.trk = Trk()
    b.dma(SP, k.trif[:], c.tri_d, writes=[k.trk])
    b.dma(POOL, k.maskb[:], c.tri_d, writes=[k.trk])
    b.dma(POOL, k.ident[:], c.ident_d, writes=[k.trk])
    b.memset(DVE, k.onesf[:], 1.0, [k.trk])
    return k


def hg_oml(c, lbrep_d):
    b = c.b
    c.oml = b.sb([128, 2, 1024], F32, "oml")
    c.oml_trk = Trk()
    with c.arena as es:
        e = b.sb([128, 8, 1024], F32, "lbe", es)
        e_trk = Trk()
        for j in range(8):
            b.dma(SP, e[:, j, :], lbrep_d[:, j, :], writes=[e_trk])
        for j in range(8):
            b.act(e[:, j, :], e[:, j, :], AF.Exp, [e_trk], [e_trk])
        for dr in range(2):
            den = c.oml[:, dr, :]
            b.tt(DVE, den, e[:, dr * 4, :], e[:, dr * 4 + 1, :], ALU.add, [e_trk], [c.oml_trk])
            b.tt(DVE, den, den, e[:, dr * 4 + 2, :], ALU.add, [e_trk, c.oml_trk], [c.oml_trk])
            b.tt(DVE, den, den, e[:, dr * 4 + 3, :], ALU.add, [e_trk, c.oml_trk], [c.oml_trk])
            b.recip(den, den, [c.oml_trk], [c.oml_trk])
            b.tt(DVE, den, den, e[:, dr * 4, :], ALU.mult, [e_trk, c.oml_trk], [c.oml_trk])
        b.barrier()


def hg_project(c, ntile, wqv_d, wf_d, scr_qv, scr_f, scr_trk, wog_d=None, scr_og=None):
    b = c.b
    TT = ntile * 128
    with c.arena as es:
        hT = b.sb([128, 8, TT], BF16, "hT", es)
        h_trk = Trk()
        wts = b.sb([128, 8 * 2048], BF16, "wts", es)
        w_trk = Trk()
        sgb = [(b.sb([128, 512], BF16, "sgb", es), Trk()) for _ in range(2)]
        sgf = [(b.sb([128, 512], F32, "sgf", es), Trk()) for _ in range(2)]
        st = {"i": 0}
        tl = tiles_of(TT)
        norm_mod(c, hT, h_trk, 0, [(t0, n, (1 if t0 >= NT else 0), t0) for (t0, n) in tl])
        if wog_d is not None:
            def evac_og(f, ti, t0, n, ps, ps_trk):
                sg, sg_trk = sgb[st["i"] % 2]
                st["i"] += 1
                b.act(sg[:, 0:n], ps[:, 0:n], AF.Silu, [ps_trk], [sg_trk])
                b.dma(SP, scr_og[f, :, t0:t0 + n], sg[:, 0:n], reads=[sg_trk], writes=[scr_trk], st=sg_trk)
            linear_fm(c, wog_d, 8, 1, 8, hT, h_trk, tl, evac_og)
        for half, (wd, scr, isf) in enumerate([(wqv_d, scr_qv, False), (wf_d, scr_f, True)]):
            for k in range(8):
                b.dma(POOL, wts[:, k * 2048:(k + 1) * 2048], wd[:, k * 2048:(k + 1) * 2048], writes=[w_trk])
            for kt in range(ntile):
                tk = kt * 128
                for fc in range(4):
                    ps, ps_trk = b.nextbank()
                    for k in range(8):
                        b.mm(ps[:, 0:512], hT[:, k, tk:tk + 128], wts[:, k * 2048 + fc * 512:k * 2048 + fc * 512 + 512],
                             k == 0, k == 7, [h_trk, w_trk], [ps_trk], inc=(k == 7))
                    sg, sg_trk = (sgf if isf else sgb)[st["i"] % 2]
                    st["i"] += 1
                    if st["i"] % 2:
                        b.copy(DVE, sg[:], ps[:, 0:512], [ps_trk], [sg_trk])
                    else:
                        b.op(ACT, lambda e, o=sg[:], i=ps[:, 0:512]: e.copy(o, i), reads=[ps_trk], writes=[sg_trk])
                    b.dma(SP, scr[kt, :, fc * 512:(fc + 1) * 512], sg[:], reads=[sg_trk], writes=[scr_trk], st=sg_trk)
        b.barrier()


def hg_alloc_wide(c, es, with_q):
    b = c.b
    w = Ctx()
    w.fp = [(b.sb([128, 1024], F32, "fp", es), Trk()) for _ in range(2)]
    w.qv = [(b.sb([128, 2048], BF16, "qv", es), Trk()) for _ in range(2)]
    w.L = b.sb([128, 1024], F32, "L", es)
    w.Enb = b.sb([128, 1024], F32, "Enb", es)
    w.Etot = b.sb([128, 1024], F32, "Etot", es)
    w.kh = b.sb([128, 1024], BF16, "kh", es)
    w.wt = Trk()
    if with_q:
        w.Eb = b.sb([128, 1024], F32, "Eb", es)
        w.qs = b.sb([128, 1024], BF16, "qs", es)
        w.qt = b.sb([128, 1024], BF16, "qt", es)
        w.kt = b.sb([128, 1024], BF16, "kt", es)
    w.i = 0
    return w


def hg_wide(c, w, k, dr, kt, scr_qv, scr_f, scr_trk, with_q):
    b = c.b
    fp, fp_trk = w.fp[w.i % 2]
    qv, qv_trk = w.qv[w.i % 2]
    w.i += 1
    b.dma(SP, fp[:], scr_f[kt, :, dr * 1024:(dr + 1) * 1024], reads=[scr_trk], writes=[fp_trk])
    b.dma(SP, qv[:], scr_qv[kt], reads=[scr_trk], writes=[qv_trk])
    b.act(fp[:], fp[:], AF.Sigmoid, [fp_trk], [fp_trk], scale=-1.0)
    b.tt(DVE, fp[:], fp[:], c.oml[:, dr, :], ALU.mult, [fp_trk, c.oml_trk], [fp_trk])
    b.act(w.L[:], fp[:], AF.Ln, [fp_trk], [w.wt], bias=1.0, scale=-1.0)
    bps = b.pst[0]
    tps = b.pst[1]
    b_trks = [b.bank_trk[0], b.bank_trk[1]]
    t_trks = [b.bank_trk[2], b.bank_trk[3]]
    for hf in range(2):
        b.mm(bps[:, hf * 512:(hf + 1) * 512], k.trif[:, dr, :], w.L[:, hf * 512:(hf + 1) * 512], True, True,
             [k.trk, w.wt], b_trks, inc=False)
    for hf in range(2):
        b.mm(tps[:, hf * 512:(hf + 1) * 512], k.onesf[:], w.L[:, hf * 512:(hf + 1) * 512], True, True,
             [k.trk, w.wt], t_trks, inc=(hf == 1))
    b.act(w.Enb[:], bps[:], AF.Exp, b_trks, [w.wt], scale=-1.0)
    b.act(w.Etot[:], tps[:], AF.Exp, t_trks, [w.wt])
    if with_q:
        b.act(w.Eb[:], bps[:], AF.Exp, b_trks, [w.wt])
        b.act(w.qs[:], qv[:, 0:1024], AF.Silu, [qv_trk], [w.wt])
        b.tt(DVE, w.qt[:], w.qs[:], w.Eb[:], ALU.mult, [w.wt], [w.wt])
    b.tt(DVE, w.Enb[:], w.Enb[:], fp[:], ALU.mult, [w.wt, fp_trk], [w.wt])
    if with_q:
        b.copy(DVE, w.kt[:], w.Enb[:], [w.wt], [w.wt])
    b.tt(DVE, w.kh[:], w.Enb[:], w.Etot[:], ALU.mult, [w.wt], [w.wt])
    return qv, qv_trk


def hg_state_update(c, w, k, S, s_trk, decT, dec_trk, hd, h, v_ap, v_trk, Lsum=None, l_trk=None):
    b = c.b
    dps, dps_trk = b.bank(4)
    b.mm(dps[:, hd:hd + 1], w.L[:, h * 128:(h + 1) * 128], k.onesf[:, 0:1], True, True, [w.wt, k.trk], [dps_trk],
         inc=True)
    b.act(decT[:, hd:hd + 1], dps[:, hd:hd + 1], AF.Exp, [dps_trk], [dec_trk])
    if Lsum is not None:
        b.tt(DVE, Lsum[:, hd:hd + 1], Lsum[:, hd:hd + 1], dps[:, hd:hd + 1], ALU.add, [dps_trk, l_trk], [l_trk])
    sps, sps_trk = b.bank(5 + hd % 2)
    b.mm(sps[:, 0:128], w.kh[:, h * 128:(h + 1) * 128], v_ap, True, True, [w.wt, v_trk], [sps_trk], inc=True)
    b.stt(DVE, S[:, hd, :], S[:, hd, :], decT[:, hd:hd + 1], sps[:, 0:128], ALU.mult, ALU.add,
          [s_trk, dec_trk, sps_trk], [s_trk])


def hg_din(b, c):
    c.tri_d = b.din("tri", [128, 2, 128])
    c.ident_d = b.din("ident", [128, 128])


def build_A3():
    b = Builder()
    TT = NT + NCX
    xT_d = b.din("xT", [128, 8, TT])
    d = common_inputs(b)
    wqv_d = b.din("wqv", [128, 8 * 2048])
    wf_d = b.din("wf", [128, 8 * 2048])
    lbrep_d = b.din("lbrep", [128, 8, 1024])
    sloc_o, sloc_trk = b.dout("S_loc", [128, 16, 128])
    sctx_o, sctx_trk = b.dout("S_ctx", [128, 16, 128])
    dloc_o, dloc_trk = b.dout("D_loc", [128, 16])
    mod_o, mod_trk = b.dout("modT_o", [128, 96])
    scr_qv = b.dscr("scr_qv", [18, 128, 2048], BF16)
    scr_f = b.dscr("scr_f", [18, 128, 2048], F32)
    scr_trk = Trk()
    c = setup_common(b, TT, 20500)
    hg_din(b, c)
    for ch in range(8):
        b.dma(SP, c.xT[:, ch, :], xT_d[:, ch, :], writes=[c.x_trk])
    hg_oml(c, lbrep_d)
    with c.arena as es:
        compute_mods(c, d["ada_w"], d["ada_bT"], d["ccT"], es)
        make_AB(c, d["ng1"], 0, es)
        b.dma(SP, mod_o, c.modT[:].rearrange("p j w -> p (j w)"), reads=[c.mod_trk], writes=[mod_trk], st=c.mod_trk)
        b.barrier()
    hg_project(c, 18, wqv_d, wf_d, scr_qv, scr_f, scr_trk)
    with c.arena as es:
        k = hg_consts(c, es, b)
        w = hg_alloc_wide(c, es, False)
        S = b.sb([128, 16, 128], F32, "S", es)
        s_trk = Trk()
        decT = b.sb([128, 16], F32, "decT", es)
        dec_trk = Trk()
        Lsum = b.sb([128, 16], F32, "Lsum", es)
        l_trk = Trk()
        b.memset(DVE, S[:], 0.0, [s_trk])
        for phase in range(2):
            for dr in range(2):
                if phase == 0:
                    order = [16, 17] if dr == 0 else [17, 16]
                else:
                    order = list(range(16)) if dr == 0 else list(range(15, -1, -1))
                for kt in order:
                    qv, qv_trk = hg_wide(c, w, k, dr, kt, scr_qv, scr_f, scr_trk, False)
                    for h in range(8):
                        hg_state_update(c, w, k, S, s_trk, decT, dec_trk, dr * 8 + h, h,
                                        qv[:, 1024 + h * 128:1024 + (h + 1) * 128], qv_trk,
                                        Lsum if phase == 1 else None, l_trk)
            if phase == 0:
                b.dma(SP, sctx_o, S[:], reads=[s_trk], writes=[sctx_trk], st=s_trk)
                b.memset(DVE, S[:], 0.0, [s_trk])
                b.memset(DVE, Lsum[:], 0.0, [l_trk])
        b.act(Lsum[:], Lsum[:], AF.Exp, [l_trk], [l_trk])
        b.dma(SP, sloc_o, S[:], reads=[s_trk], writes=[sloc_trk], st=s_trk)
        b.dma(SP, dloc_o, Lsum[:], reads=[l_trk], writes=[dloc_trk], st=l_trk)
        b.barrier()
    return b.finish()


def build_B3():
    b = Builder()
    TT = NT
    xT_d = b.din("xT", [128, 8, TT])
    d = common_inputs(b)
    mod_d = b.din("modT", [128, 96])
    wqv_d = b.din("wqv", [128, 8 * 2048])
    wf_d = b.din("wf", [128, 8 * 2048])
    wog_d = b.din("wog", [8, 128, 1024])
    lbrep_d = b.din("lbrep", [128, 8, 1024])
    sctx_d = b.din("S_ctx", [128, 16, 128])
    dp_d = b.din("Dp", [128, 7, 16])
    sp_d = b.din("Sp", [7, 128, 16, 128])
    gn_d = b.din("gnorm", [128, 1])
    wo_d = b.din("wo", [8, 128, 1024])
    fg_d = b.din("final_g", [128, 8])
    out_d, out_trk = b.dout("outT", [128, 8, NT])
    scr_qv = b.dscr("scr_qv", [16, 128, 2048], BF16)
    scr_f = b.dscr("scr_f", [16, 128, 2048], F32)
    scr_og = b.dscr("scr_og", [8, 128, NT], BF16)
    scr_trk = Trk()
    c = setup_common(b, TT, 25600)
    hg_din(b, c)
    for ch in range(8):
        b.dma(SP, c.xT[:, ch, :], xT_d[:, ch, :], writes=[c.x_trk])
    b.dma(SP, c.modT[:].rearrange("p j w -> p (j w)"), mod_d, writes=[c.mod_trk])
    hg_oml(c, lbrep_d)
    with c.arena as es:
        make_AB(c, d["ng1"], 0, es)
        b.barrier()
    hg_project(c, 16, wqv_d, wf_d, scr_qv, scr_f, scr_trk, wog_d, scr_og)
    with c.arena as es:
        k = hg_consts(c, es, b)
        w = hg_alloc_wide(c, es, True)
        S = b.sb([128, 16, 128], F32, "S", es)
        s_trk = Trk()
        Sb = b.sb([128, 16, 128], BF16, "Sb", es)
        sb_trk = Trk()
        decT = b.sb([128, 16], F32, "decT", es)
        dec_trk = Trk()
        oT = b.sb([128, 8, NT], BF16, "oT", es)
        o_trk = Trk()
        dp = b.sb([128, 7, 16], F32, "dp", es)
        dp_trk = Trk()
        gn = b.sb([128, 1], F32, "gn", es)
        gn_trk = Trk()
        ogr = [(b.sb([128, 128], BF16, "og", es), Trk()) for _ in range(2)]
        tr = [(b.sb([128, 2, 128], BF16, "tr", es), Trk()) for _ in range(2)]
        am = [(b.sb([128, 128], BF16, "am", es), Trk()) for _ in range(2)]
        of = [(b.sb([128, 128], F32, "of", es), Trk()) for _ in range(2)]
        sq = [(b.sb([128, 128], BF16, "sqo", es), Trk()) for _ in range(2)]
        rs = [(b.sb([128, 128], F32, "rs", es), Trk()) for _ in range(2)]
        b.dma(SP, dp[:], dp_d, writes=[dp_trk])
        b.dma(SP, gn[:], gn_d, writes=[gn_trk])
        b.dma(SP, S[:], sctx_d, writes=[s_trk])
        for step in range(7):
            for half in range(2):
                stg, stg_trk = w.fp[(step * 2 + half) % 2]
                b.dma(SP, stg[:], sp_d[step, :, half * 8:(half + 1) * 8, :].rearrange("p h e -> p (h e)"), writes=[stg_trk])
                for hh in range(8):
                    hd = half * 8 + hh
                    b.stt(DVE, S[:, hd, :], S[:, hd, :], dp[:, step, hd:hd + 1], stg[:, hh * 128:(hh + 1) * 128],
                          ALU.mult, ALU.add, [s_trk, dp_trk, stg_trk], [s_trk])
        b.copy(DVE, Sb[:], S[:], [s_trk], [sb_trk])
        cnt = {"u": 0}
        for dr in range(2):
            order = list(range(16)) if dr == 0 else list(range(15, -1, -1))
            for kt in order:
                qv, qv_trk = hg_wide(c, w, k, dr, kt, scr_qv, scr_f, scr_trk, True)
                for h in range(8):
                    hd = dr * 8 + h
                    u = cnt["u"]
                    cnt["u"] += 1
                    hs = slice(h * 128, (h + 1) * 128)
                    tp, tp_trk = b.bank(6)
                    b.mm(tp[:, 0:128], w.kt[:, hs], k.ident[:], True, True, [w.wt, k.trk], [tp_trk], inc=False)
                    b.mm(tp[:, 128:256], w.qt[:, hs], k.ident[:], True, True, [w.wt, k.trk], [tp_trk], inc=True)
                    t_, t_trk = tr[u % 2]
                    b.op(ACT, lambda e, o=t_[:].rearrange("p a s -> p (a s)"), i=tp[:, 0:256]: e.copy(o, i),
                         reads=[tp_trk], writes=[t_trk])
                    ap_, ap_trk = b.bank(7)
                    b.mm(ap_[:, 0:128], t_[:, 0, :], t_[:, 1, :], True, True, [t_trk], [ap_trk], inc=True)
                    a_, a_trk = am[u % 2]
                    b.tt(DVE, a_[:], ap_[:, 0:128], k.maskb[:, dr, :], ALU.mult, [ap_trk, k.trk], [a_trk])
                    op_, op_trk = b.bank(5 + 0)
                    op_, op_trk = b.pst[3][:, 512 + 256:512 + 384], b.bank_trk[7]
                    b.mm(op_, qv[:, 1024 + h * 128:1024 + (h + 1) * 128], a_[:], True, False, [qv_trk, a_trk], [op_trk],
                         inc=False)
                    b.mm(op_, Sb[:, hd, :], t_[:, 1, :], False, True, [sb_trk, t_trk], [op_trk], inc=True)
                    if dr == 0:
                        b.op(ACT, lambda e, o=oT[:, h, kt * 128:(kt + 1) * 128], i=op_: e.copy(o, i),
                             reads=[op_trk], writes=[o_trk])
                    else:
                        o_, of_trk = of[u % 2]
                        b.tt(DVE, o_[:], op_, oT[:, h, kt * 128:(kt + 1) * 128], ALU.add, [op_trk, o_trk], [of_trk])
                        s_, sq_trk = sq[u % 2]
                        b.act(s_[:], o_[:], AF.Square, [of_trk], [sq_trk])
                        np_, np_trk = b.pst[3][:, 512 + 384:512 + 512], b.bank_trk[7]
                        b.mm(np_, c.ones[:], s_[:], True, True, [c.ones_trk, sq_trk], [np_trk], inc=True)
                        r_, r_trk = rs[u % 2]
                        b.act(r_[:], np_, AF.Sqrt, [np_trk, c.eps_trk], [r_trk], bias=c.eps[:, 0:1], scale=1.0 / 128)
                        b.recip(r_[:], r_[:], [r_trk], [r_trk])
                        g_, g_trk = ogr[u % 2]
                        b.dma(SP, g_[:], scr_og[h, :, kt * 128:(kt + 1) * 128], reads=[scr_trk], writes=[g_trk])
                        b.tt(DVE, o_[:], o_[:], r_[:], ALU.mult, [of_trk, r_trk], [of_trk])
                        b.stt(DVE, oT[:, h, kt * 128:(kt + 1) * 128], o_[:], gn[:, 0:1], g_[:], ALU.mult, ALU.mult,
                              [of_trk, gn_trk, g_trk], [o_trk])
                    hg_state_update(c, w, k, S, s_trk, decT, dec_trk, hd, h,
                                    qv[:, 1024 + h * 128:1024 + (h + 1) * 128], qv_trk)
                    b.op(ACT, lambda e, o=Sb[:, hd, :], i=S[:, hd, :]: e.copy(o, i), reads=[s_trk], writes=[sb_trk])

        def evac_out(f, ti, t0, n, ps, ps_trk):
            b.stt(DVE, c.xT[:, f, t0:t0 + n], ps[:, 0:n], c.modT[:, 16 + f, 0:1], c.xT[:, f, t0:t0 + n],
                  ALU.mult, ALU.add, [ps_trk, c.mod_trk, c.x_trk], [c.x_trk])

        linear_fm(c, wo_d, 8, 1, 8, oT, o_trk, tiles_of(NT), evac_out, banks=(0, 4))
        b.barrier()
    ffn(c, d["ng2"], d["ffn_w_in"], d["ffn_w_out"], supers=[[(0, 512, 0), (512, 512, 0)], [(1024, 512, 0), (1536, 512, 0)]])
    with c.arena as es:
        fg = b.sb([128, 8], F32, "fg", es)
        c.g_trk = Trk()
        b.dma(SP, fg[:], fg_d, writes=[c.g_trk])
        ot = b.sb([128, 8, NT], F32, "ot", es)
        ot_trk = Trk()
        norm_mod(c, ot, ot_trk, 0, [(t0, n, 0, t0) for (t0, n) in tiles_of(NT)], plain_g=fg)
        for ch in range(8):
            b.dma(SP, out_d[:, ch, :], ot[:, ch, :], reads=[ot_trk], writes=[out_trk], st=ot_trk)
    return b.finish()


def hg_host_common(inp):
    W = inp["hg_w_in"][0]
    q, ffw, fbw, v, og = [W[:, i * 1024:(i + 1) * 1024] for i in range(5)]

    def tmaj(M):
        return np.ascontiguousarray(M.reshape(8, 128, M.shape[1]).transpose(1, 0, 2).reshape(128, 8 * M.shape[1]))
    s = np.arange(128)[:, None]
    t = np.arange(128)[None, :]
    tri = np.stack([(s <= t), (s >= t)], axis=1).astype(np.float32)
    lb = inp["hg_lb"].reshape(8, 1024)
    return {
        "wqv": tmaj(np.concatenate([q, v], axis=1)),
        "wf": tmaj(np.concatenate([ffw, fbw], axis=1)),
        "wog": lhsT_blocks(og, 1),
        "lbrep": np.ascontiguousarray(np.broadcast_to(lb[None], (128, 8, 1024))),
        "tri": np.ascontiguousarray(tri),
        "ident": np.eye(128, dtype=np.float32),
    }


def run_layer3(xs, ctx, inp):
    common = common_host(inp, 3)
    hc = hg_host_common(inp)
    a_common = dict(common)
    a_common.update({k: hc[k] for k in ("wqv", "wf", "lbrep", "tri", "ident")})
    maps = []
    for cid in range(NCORES):
        lo, hi = cid * NT, (cid + 1) * NT
        m = dict(a_common)
        m["xT"] = toT(np.concatenate([xs[lo:hi], ctx], axis=0))
        maps.append(m)
    resA = run_bass_kernel_spmd(build_A3(), maps, core_ids=list(range(NCORES))).results
    b_common = dict(common)
    b_common.update(hc)
    b_common.update({
        "S_ctx": resA[0]["S_ctx"],
        "gnorm": np.ascontiguousarray(inp["hg_gnorm_g"][0].reshape(128, 1)),
        "wo": lhsT_blocks(inp["hg_wo"][0], 1),
        "final_g": fm(inp["final_g"]),
    })
    one = np.ones((128, 8), np.float32)
    zero = np.zeros((128, 8, 128), np.float32)
    mapsB = []
    for cid in range(NCORES):
        fw = [None] * (7 - cid) + list(range(0, cid))
        bw = [None] * cid + list(range(NCORES - 1, cid, -1))
        Dp = np.zeros((128, 7, 16), np.float32)
        Sp = np.zeros((7, 128, 16, 128), np.float32)
        for st in range(7):
            Dp[:, st, 0:8] = one if fw[st] is None else resA[fw[st]]["D_loc"][:, 0:8]
            Dp[:, st, 8:16] = one if bw[st] is None else resA[bw[st]]["D_loc"][:, 8:16]
            Sp[st, :, 0:8] = zero if fw[st] is None else resA[fw[st]]["S_loc"][:, 0:8]
            Sp[st, :, 8:16] = zero if bw[st] is None else resA[bw[st]]["S_loc"][:, 8:16]
        m = dict(b_common)
        m["xT"] = toT(xs[cid * NT:(cid + 1) * NT])
        m["modT"] = resA[cid]["modT_o"]
        m["Dp"], m["Sp"] = Dp, Sp
        mapsB.append(m)
    resB = run_bass_kernel_spmd(build_B3(), mapsB, core_ids=list(range(NCORES))).results
    return np.concatenate([fromT(r["outT"]) for r in resB], axis=0)


def kernel(**inp):
    inp = {k: np.asarray(v) for k, v in inp.items()}
    xs = np.ascontiguousarray(inp["x"][0], dtype=np.float32)
    ctx = np.ascontiguousarray(inp["ctx"][0], dtype=np.float32)
    xs, ctx = run_layer0(xs, ctx, inp)
    xs, ctx = run_layer1(xs, ctx, inp)
    xs, ctx = run_layer2(xs, ctx, inp)
    out = run_layer3(xs, ctx, inp)
    return np.ascontiguousarray(out[None].astype(np.float32))
```
